# Optimizing a Trainium2 kernel written in Bass

```python
import math
import jax, jax.numpy as jnp
from jax import lax
import numpy as np

D_MODEL = 1024
BATCH = 4
SEQ = 8192
DEPTH = 2

N_MIXERS = 2
S5_GROUP = 16
S5_GROUPS = D_MODEL // S5_GROUP
S5_STATE = 64
SCAN_CHUNK = 128
DT_MIN = 1e-3
DT_MAX = 1e-1
CONV_WIDTH = 3
D_FF = ((11 * D_MODEL // 4 + 127) // 128) * 128
N_A_LAYERS = (DEPTH + 1) // 2
N_B_LAYERS = DEPTH // 2
RMS_EPS = 1e-6

kernel_name = "hybrid_s5_shortconv_convffn"


def rmsnorm(x, g):
    xf = x.astype(jnp.float32)
    y = xf * lax.rsqrt(jnp.mean(xf * xf, axis=-1, keepdims=True) + RMS_EPS)
    return (y * g.astype(jnp.float32)).astype(x.dtype)


def causal_dwconv(x, w):
    k_w = w.shape[0]
    seq = x.shape[1]
    xp = jnp.pad(x, ((0, 0), (k_w - 1, 0), (0, 0)))
    y = xp[:, 0:seq] * w[0]
    for k in range(1, k_w):
        y = y + xp[:, k:k + seq] * w[k]
    return y


def _ssm_combine(left, right):
    ar_l, ai_l, br_l, bi_l = left
    ar_r, ai_r, br_r, bi_r = right
    return (ar_r * ar_l - ai_r * ai_l,
            ar_r * ai_l + ai_r * ar_l,
            ar_r * br_l - ai_r * bi_l + br_r,
            ar_r * bi_l + ai_r * br_l + bi_r)


def s5_mixer(u, a_re, a_im, log_dt, b_re, b_im, c_re, c_im, d_skip, w_glu):
    f32 = jnp.float32
    bsz, seq, _ = u.shape
    lam_r = a_re.astype(f32)
    lam_i = a_im.astype(f32)
    dt = jnp.exp(log_dt.astype(f32))[:, None]
    mag = jnp.exp(lam_r * dt)
    ab_r = mag * jnp.cos(lam_i * dt)
    ab_i = mag * jnp.sin(lam_i * dt)
    den = lam_r * lam_r + lam_i * lam_i
    nr = ab_r - 1.0
    g_r = ((nr * lam_r + ab_i * lam_i) / den)[..., None]
    g_i = ((ab_i * lam_r - nr * lam_i) / den)[..., None]
    br = b_re.astype(f32)
    bi = b_im.astype(f32)
    bb_r = g_r * br - g_i * bi
    bb_i = g_r * bi + g_i * br
    cr = c_re.astype(f32)
    ci = c_im.astype(f32)
    steps = jnp.arange(1, SCAN_CHUNK + 1, dtype=f32)[:, None, None]
    pmag = jnp.exp(lam_r * dt * steps)
    pw_r = pmag * jnp.cos(lam_i * dt * steps)
    pw_i = pmag * jnp.sin(lam_i * dt * steps)
    a_blk_r = jnp.broadcast_to(ab_r, (bsz, SCAN_CHUNK, S5_GROUPS, S5_STATE))
    a_blk_i = jnp.broadcast_to(ab_i, (bsz, SCAN_CHUNK, S5_GROUPS, S5_STATE))

    n_chunks = seq // SCAN_CHUNK
    uf = u.astype(f32)
    uc = uf.reshape(bsz, n_chunks, SCAN_CHUNK, S5_GROUPS, S5_GROUP).transpose(1, 0, 2, 3, 4)

    def chunk_step(carry, u_blk):
        h0_r, h0_i = carry
        bu_r = jnp.einsum('btgh,gph->btgp', u_blk, bb_r)
        bu_i = jnp.einsum('btgh,gph->btgp', u_blk, bb_i)
        _, _, loc_r, loc_i = lax.associative_scan(
            _ssm_combine, (a_blk_r, a_blk_i, bu_r, bu_i), axis=1)
        h_r = loc_r + pw_r * h0_r[:, None] - pw_i * h0_i[:, None]
        h_i = loc_i + pw_r * h0_i[:, None] + pw_i * h0_r[:, None]
        y = jnp.einsum('btgp,ghp->btgh', h_r, cr) - jnp.einsum('btgp,ghp->btgh', h_i, ci)
        return (h_r[:, -1], h_i[:, -1]), y

    h_init = (jnp.zeros((bsz, S5_GROUPS, S5_STATE), f32),
              jnp.zeros((bsz, S5_GROUPS, S5_STATE), f32))
    _, ys = lax.scan(chunk_step, h_init, uc)
    y = ys.transpose(1, 0, 2, 3, 4).reshape(bsz, seq, D_MODEL)
    y = y + d_skip.astype(f32) * uf
    z = jax.nn.gelu(y)
    za, zg = jnp.split(z @ w_glu.astype(f32), 2, axis=-1)
    return (za * jax.nn.sigmoid(zg)).astype(u.dtype)


def shortconv_mixer(u, w_in, conv_w, w_out):
    b_gate, c_gate, h = jnp.split(u @ w_in, 3, axis=-1)
    v = causal_dwconv(c_gate * h, conv_w)
    return (b_gate * v) @ w_out


def conv_ffn(u, w_up, conv_w, conv_b, w_down):
    g, v = jnp.split(u @ w_up, 2, axis=-1)
    g = causal_dwconv(g, conv_w) + conv_b
    return (jax.nn.silu(g) * v) @ w_down


def setup_inputs(seed: int = 0) -> dict:
    key = jax.random.key(seed)
    ks = jax.random.split(key, 24)
    f32 = jnp.float32
    d = D_MODEL
    g, p, h, f = S5_GROUPS, S5_STATE, S5_GROUP, D_FF
    na, nb = N_A_LAYERS, N_B_LAYERS
    nrm = lambda k, s, sc: jax.random.normal(k, s, f32) * sc
    a_im_base = jnp.arange(p, dtype=f32) * jnp.pi
    return {
        "x": nrm(ks[0], (BATCH, SEQ, d), 1.0),
        "norm_mix": 1.0 + nrm(ks[1], (DEPTH, d), 0.02),
        "norm_ffn": 1.0 + nrm(ks[2], (DEPTH, d), 0.02),
        "norm_final": 1.0 + nrm(ks[3], (d,), 0.02),
        "s5_a_re": -0.5 + nrm(ks[4], (na, g, p), 0.01),
        "s5_a_im": a_im_base + nrm(ks[5], (na, g, p), 0.01),
        "s5_log_dt": jax.random.uniform(ks[6], (na, g), f32,
                                        minval=math.log(DT_MIN), maxval=math.log(DT_MAX)),
        "s5_b_re": nrm(ks[7], (na, g, p, h), (2.0 * h) ** -0.5),
        "s5_b_im": nrm(ks[8], (na, g, p, h), (2.0 * h) ** -0.5),
        "s5_c_re": nrm(ks[9], (na, g, h, p), (1.0 * p) ** -0.5),
        "s5_c_im": nrm(ks[10], (na, g, h, p), (1.0 * p) ** -0.5),
        "s5_d": nrm(ks[11], (na, d), 1.0),
        "s5_w_glu": nrm(ks[12], (na, d, 2 * d), d ** -0.5),
        "sc_w_in": nrm(ks[13], (nb, d, 3 * d), d ** -0.5),
        "sc_conv_w": nrm(ks[14], (nb, CONV_WIDTH, d), CONV_WIDTH ** -0.5),
        "sc_w_out": nrm(ks[15], (nb, d, d), d ** -0.5),
        "ffn_w_up": nrm(ks[16], (DEPTH, d, 2 * f), d ** -0.5),
        "ffn_conv_w": nrm(ks[17], (DEPTH, CONV_WIDTH, f), CONV_WIDTH ** -0.5),
        "ffn_conv_b": nrm(ks[18], (DEPTH, f), 0.01),
        "ffn_w_down": nrm(ks[19], (DEPTH, f, d), f ** -0.5),
    }


def reference(x, norm_mix, norm_ffn, norm_final, s5_a_re, s5_a_im, s5_log_dt,
              s5_b_re, s5_b_im, s5_c_re, s5_c_im, s5_d, s5_w_glu,
              sc_w_in, sc_conv_w, sc_w_out,
              ffn_w_up, ffn_conv_w, ffn_conv_b, ffn_w_down):
    h = x
    for i in range(DEPTH):
        j = i // N_MIXERS
        u = rmsnorm(h, norm_mix[i])
        if i % N_MIXERS == 0:
            mix = s5_mixer(u, s5_a_re[j], s5_a_im[j], s5_log_dt[j], s5_b_re[j], s5_b_im[j],
                           s5_c_re[j], s5_c_im[j], s5_d[j], s5_w_glu[j])
        else:
            mix = shortconv_mixer(u, sc_w_in[j], sc_conv_w[j], sc_w_out[j])
        h = h + mix.astype(h.dtype)
        u = rmsnorm(h, norm_ffn[i])
        h = h + conv_ffn(u, ffn_w_up[i], ffn_conv_w[i], ffn_conv_b[i], ffn_w_down[i]).astype(h.dtype)
    return rmsnorm(h, norm_final)
```

```python
import contextlib
import numpy as np
import concourse.bass as bass
import concourse.mybir as mybir
from concourse.bass_utils import run_bass_kernel_spmd

F32 = mybir.dt.float32
BF16 = mybir.dt.bfloat16
AF = mybir.ActivationFunctionType
ALU = mybir.AluOpType

D = 1024
KT = 8
FF = 2816
FT = 22
T = 512
L = 8
NCH = T // L
NS = 5
SLOT = 4096
RMS_EPS = 1e-6
ENGS = ("pe", "act", "dve", "pool", "sp")

B_GLU, B_UP0, B_DN0, B_IN, B_OUT, B_UP1, B_DN1, NBLK = 0, 4, 15, 23, 31, 33, 44, 52
C_NM0, C_NM1, C_NF0, C_NF1, C_NFIN, C_D, C_SCW, C_FW, C_FB, NCV = 0, 8, 16, 24, 32, 40, 48, 72, 204, 248


class Prog:
    def __init__(self, nc):
        self.nc = nc
        self.cnt = {e: 0 for e in ENGS}
        self.ops = {e: [] for e in ENGS}
        self.writer = {}
        self.readers = {}
        self.seen = {e: {} for e in ENGS}
        self.pending = {e: {} for e in ENGS}
        self.chan_cnt = {}
        self.chan_events = {}
        self.sems = {}
        self.alias = {}

    def _need(self, eng, key, val, waits):
        if key == eng and eng == "pe":
            return
        if self.seen[eng].get(key, 0) >= val:
            return
        waits[key] = max(waits.get(key, 0), val)

    def op(self, eng, fn, reads=(), writes=(), chan=None, after=()):
        reads = [k for n in reads for k in self.alias.get(n, (n,))]
        writes = [k for n in writes for k in self.alias.get(n, (n,))]
        waits = {}
        for ev in after:
            self._need(eng, ev[0], ev[1], waits)
        for k, v in self.pending[eng].items():
            self._need(eng, k, v, waits)
        self.pending[eng] = {}
        for b in reads:
            ev = self.writer.get(b)
            if ev is not None:
                self._need(eng, ev[0], ev[1], waits)
        for b in writes:
            ev = self.writer.get(b)
            if ev is not None:
                self._need(eng, ev[0], ev[1], waits)
            for ev in self.readers.get(b, ()):
                self._need(eng, ev[0], ev[1], waits)
        for k, v in waits.items():
            self.seen[eng][k] = v
        if chan is None:
            self.cnt[eng] += 1
            ev = [eng, self.cnt[eng]]
        else:
            key = "c:" + chan
            self.chan_cnt[key] = self.chan_cnt.get(key, 0) + 16
            ev = [key, self.chan_cnt[key]]
            self.chan_events.setdefault(key, []).append(ev)
        for b in writes:
            self.writer[b] = ev
            self.readers[b] = []
        for b in reads:
            if b not in writes:
                self.readers.setdefault(b, []).append(ev)
        self.ops[eng].append((waits, fn, ev))
        return ev

    def seal(self, chan):
        key = "c:" + chan
        tot = self.chan_cnt.get(key, 0)
        for ev in self.chan_events.get(key, []):
            ev[1] = tot

    def barrier(self):
        snap = dict(self.cnt)
        snap.update({k: v for k, v in self.chan_cnt.items() if not k.startswith("c:cast")})
        for e in ENGS:
            p = self.pending[e]
            for k, v in snap.items():
                if v > 0 and k != e:
                    p[k] = max(p.get(k, 0), v)

    def emit(self):
        nc = self.nc
        keys = sorted(set(ENGS) | set(self.chan_cnt.keys()))
        with contextlib.ExitStack() as st:
            for i, k in enumerate(keys):
                self.sems[k] = st.enter_context(nc.semaphore("s%d" % i))
            fin = {e: self.cnt[e] for e in ENGS if e != "sp" and self.cnt[e] > 0}
            fin.update(self.chan_cnt)
            with nc.Block() as block:
                def run(eng_name, eng):
                    for waits, fn, ev in self.ops[eng_name]:
                        for k, v in waits.items():
                            eng.wait_ge(self.sems[k], v)
                        ins = fn(eng)
                        ins.then_inc(self.sems[ev[0]], 16 if ev[0].startswith("c:") else 1)
                    if eng_name == "sp":
                        for k, v in fin.items():
                            eng.wait_ge(self.sems[k], v)

                @block.tensor
                def _(eng):
                    run("pe", eng)

                @block.scalar
                def _(eng):
                    run("act", eng)

                @block.vector
                def _(eng):
                    run("dve", eng)

                @block.gpsimd
                def _(eng):
                    run("pool", eng)

                @block.sync
                def _(eng):
                    run("sp", eng)


def _nm(*aps):
    return [a.name for a in aps if hasattr(a, "name") and not isinstance(a, (int, float))]


class B:
    def __init__(self, P):
        self.P = P

    def tt(self, eng, out, in0, in1, op, rk=None, wk=None):
        self.P.op(eng, lambda e: e.tensor_tensor(out=out, in0=in0, in1=in1, op=op),
                  reads=_nm(in0, in1) if rk is None else rk, writes=_nm(out) if wk is None else wk)

    def ts(self, eng, out, in0, s1, op0, s2=None, op1=None):
        if op1 is None:
            s2, op1 = 0.0, ALU.add
        fn = lambda e: e.tensor_scalar(out=out, in0=in0, scalar1=s1, scalar2=s2, op0=op0, op1=op1)
        self.P.op(eng, fn, reads=_nm(in0, s1, s2), writes=_nm(out))

    def stt(self, out, in0, scalar, in1, op0, op1):
        self.P.op("dve", lambda e: e.scalar_tensor_tensor(out=out, in0=in0, scalar=scalar, in1=in1, op0=op0, op1=op1),
                  reads=_nm(in0, scalar, in1), writes=_nm(out))

    def act(self, out, in_, func, bias=None, scale=None, wk=None):
        kw = {}
        if bias is not None:
            kw["bias"] = bias
        if scale is not None:
            kw["scale"] = scale
        self.P.op("act", lambda e: e.activation(out=out, in_=in_, func=func, **kw),
                  reads=_nm(in_, bias, scale), writes=_nm(out) if wk is None else wk)

    def copy(self, eng, out, in_):
        if eng == "act":
            return self.act(out, in_, AF.Copy)
        self.P.op(eng, lambda e: e.tensor_copy(out=out, in_=in_), reads=_nm(in_), writes=_nm(out))

    def recip(self, out, in_):
        self.P.op("dve", lambda e: e.reciprocal(out=out, in_=in_), reads=_nm(in_), writes=_nm(out))

    def memset(self, eng, ap, val):
        self.P.op(eng, lambda e: e.memset(ap, val), writes=_nm(ap))

    def dma(self, eng, out, in_, chan, after=()):
        return self.P.op(eng, lambda e: e.dma_start(out=out, in_=in_), reads=_nm(in_), writes=_nm(out), chan=chan,
                         after=after)

    def mm(self, mms):
        reads, writes = [], []
        for m in mms:
            writes += _nm(m[0])
            reads += _nm(m[1], m[2])

        def fn(e):
            ins = None
            for (out, lhsT, rhs, start, stop, tp) in mms:
                if tp is None:
                    ins = e.matmul(out, lhsT, rhs, start=start, stop=stop)
                else:
                    ins = e.matmul(out, lhsT, rhs, start=start, stop=stop, tile_position=tp)
            return ins
        self.P.op("pe", fn, reads=sorted(set(reads)), writes=sorted(set(writes)))


def build_program(npre, nmain):
    ntile = npre + 1 + nmain
    ntok = ntile * T
    nc = bass.Bass("TRN2", target_bir_lowering=False)
    dt_in = lambda name, shape: nc.dram_tensor(name, shape, F32, kind="ExternalInput").ap()
    xT = dt_in("xT", [D, ntok])
    wst = dt_in("wst", [NBLK, 128, SLOT])
    cvec_d = dt_in("cvec", [128, NCV])
    cst_d = dt_in("cst", [128, 264])
    s5A_d = dt_in("s5A", [5, 128, 512])
    s5Ps_d = dt_in("s5Ps", [3, 128, 32])
    s5Pc_d = dt_in("s5Pc", [2, 128, 512])
    s5Qs_d = dt_in("s5Qs", [3, 64, 64])
    s5Qb_d = dt_in("s5Qb", [4, 64, 1024])
    outT = nc.dram_tensor("outT", [D, nmain * T], F32, kind="ExternalOutput").ap()
    GROUPS = [B_GLU, B_UP0, B_DN0, B_IN, B_OUT, B_UP1, B_DN1, NBLK]
    wbf_g = [nc.dram_tensor("wbf%d" % g, [GROUPS[g + 1] - GROUPS[g], 128, SLOT], BF16, kind="Internal").ap()
             for g in range(7)]

    class _WBF:
        def __getitem__(self, key):
            j = key[0] if isinstance(key, tuple) else key
            g = max(i for i in range(7) if GROUPS[i] <= j)
            blk = wbf_g[g][j - GROUPS[g]]
            return blk[key[1:]] if isinstance(key, tuple) else blk
    wbf = _WBF()
    grp_of = lambda j: max(i for i in range(7) if GROUPS[i] <= j)
    s5scr = nc.dram_tensor("s5scr", [12, 128, SLOT], BF16, kind="Internal").ap()

    P = Prog(nc)
    b = B(P)
    sb = lambda name, shape, dt=F32: nc.alloc_sbuf_tensor(name, shape, dt)

    cvec = sb("cvec_s", [128, NCV])
    cst = sb("cst_s", [128, 264])
    ones_bf = sb("ones_bf", [128, 128], BF16)
    eps_t = sb("eps_t", [128, 1])
    hpi_t = sb("hpi_t", [128, 1])
    A1 = sb("A1", [128, 2, 32])
    A2 = sb("A2", [128, 2, 32])
    PW = sb("PW", [128, 16, 2, 32])
    A1x = sb("A1x", [128, 2, 32])
    A2x = sb("A2x", [128, 2, 32])
    haloF = [[sb("haloF%d_%d" % (l, i), [128, FT, 2]) for i in range(2)] for l in range(2)]
    haloS = [sb("haloS%d" % i, [128, KT, 2]) for i in range(2)]
    ps = [nc.alloc_psum_tensor("ps%d" % i, [128, 512], F32) for i in range(8)]

    blockmask = cst[:, 0:128]
    ident = cst[:, 128:256]
    maskE = cst[:, 256:258]
    maskH = cst[:, 258:262]

    b.dma("sp", cvec[:, :], cvec_d, "const")
    b.dma("sp", cst[:, :], cst_d, "const")
    b.memset("pool", ones_bf[:, :], 1.0)
    b.memset("pool", eps_t[:, :], RMS_EPS)
    b.memset("pool", hpi_t[:, :], float(np.pi / 2))
    for l in range(2):
        for i in range(2):
            b.memset("pool", haloF[l][i][:, :, :], 0.0)
    for i in range(2):
        b.memset("pool", haloS[i][:, :, :], 0.0)
    cast_state = {"next": 0}

    def cast_some(n, after=()):
        for _ in range(n):
            j = cast_state["next"]
            if j >= NBLK:
                return
            b.dma("pool", wbf[j], wst[j], "cast%d" % grp_of(j), after=after)
            cast_state["next"] = j + 1

    with contextlib.ExitStack() as tmp:
        def tsb(name, shape, dt=F32):
            return tmp.enter_context(nc.sbuf_tensor(name, shape, dt))

        def cmul(o_r, o_i, a_r, a_i, b_r, b_i, t1, t2, eng="dve", castn=0):
            cast_some(castn)
            b.tt(eng, t1, a_r, b_r, ALU.mult)
            b.tt(eng, t2, a_i, b_i, ALU.mult)
            b.tt(eng, o_r, t1, t2, ALU.subtract)
            b.tt(eng, t1, a_r, b_i, ALU.mult)
            b.tt(eng, t2, a_i, b_r, ALU.mult)
            b.tt(eng, o_i, t1, t2, ALU.add)

        def make_a(pfx, lr, li, ldt, pp, n, want_g, eng="dve"):
            mk = lambda s: tsb(pfx + s, [128, n])[0:pp, :]
            dtt, xr, xi, zr, zi, t1, t2, t3 = [mk(s) for s in ("dt", "xr", "xi", "zr", "zi", "t1", "t2", "t3")]
            b.act(dtt, ldt, AF.Exp)
            b.tt(eng, xr, lr, dtt, ALU.mult)
            b.tt(eng, xi, li, dtt, ALU.mult)
            b.act(t1, xr, AF.Exp, scale=1.0 / 16)
            b.act(t2, xi, AF.Sin, scale=1.0 / 16)
            b.act(t3, xi, AF.Sin, scale=1.0 / 16, bias=hpi_t[0:pp, :])
            b.tt(eng, zr, t1, t3, ALU.mult)
            b.tt(eng, zi, t1, t2, ALU.mult)
            for _ in range(4):
                b.tt(eng, t1, zr, zr, ALU.mult)
                b.tt(eng, t2, zi, zi, ALU.mult)
                b.tt(eng, t3, zr, zi, ALU.mult)
                b.tt(eng, zr, t1, t2, ALU.subtract)
                b.tt(eng, zi, t3, t3, ALU.add)
            if not want_g:
                return zr, zi, None, None
            gr, gi, nr, rden = [mk(s) for s in ("gr", "gi", "nr", "rden")]
            b.tt(eng, t1, lr, lr, ALU.mult)
            b.tt(eng, t2, li, li, ALU.mult)
            b.tt(eng, t1, t1, t2, ALU.add)
            b.recip(rden, t1)
            b.ts(eng, nr, zr, -1.0, ALU.add)
            b.tt(eng, t1, nr, lr, ALU.mult)
            b.tt(eng, t2, zi, li, ALU.mult)
            b.tt(eng, t1, t1, t2, ALU.add)
            b.tt(eng, gr, t1, rden, ALU.mult)
            b.tt(eng, t1, zi, lr, ALU.mult)
            b.tt(eng, t2, nr, li, ALU.mult)
            b.tt(eng, t1, t1, t2, ALU.subtract)
            b.tt(eng, gi, t1, rden, ALU.mult)
            return zr, zi, gr, gi

        inA = [tsb("inA%d" % i, [128, 512]) for i in range(5)]
        for i in range(5):
            b.dma("sp", inA[i][:, :], s5A_d[i], "const")
        inPs = [tsb("inPs%d" % i, [128, 32]) for i in range(3)]
        for i in range(3):
            b.dma("sp", inPs[i][:, :], s5Ps_d[i], "const")
        inPc = [tsb("inPc%d" % i, [128, 512]) for i in range(2)]
        for i in range(2):
            b.dma("sp", inPc[i][:, :], s5Pc_d[i], "const")
        inQs = [tsb("inQs%d" % i, [128, 64]) for i in range(3)]
        for i in range(3):
            b.dma("sp", inQs[i][0:64, :], s5Qs_d[i], "const")
        inQb = [tsb("inQb%d" % i, [128, 1024]) for i in range(4)]
        for i in range(4):
            b.dma("sp", inQb[i][0:64, :], s5Qb_d[i], "const")
        P.seal("const")

        aAr, aAi, gAr, gAi = make_a("A_", inA[2][:, :], inA[3][:, :], inA[4][:, :], 128, 512, True)
        aPr, aPi, _, _ = make_a("P_", inPs[0][:, :], inPs[1][:, :], inPs[2][:, :], 128, 32, False)
        aQr, aQi, gQr, gQi = make_a("Q_", inQs[0][0:64, :], inQs[1][0:64, :], inQs[2][0:64, :], 64, 64, True)
        cast_some(14)
        Wr = [tsb("WrA%d" % i, [128, 512]) for i in range(2)]
        Wi = [tsb("WiA%d" % i, [128, 512]) for i in range(2)]
        tA1 = tsb("tA1", [128, 512])
        tA2 = tsb("tA2", [128, 512])
        BDt = [tsb("BDt%d" % i, [128, SLOT], BF16) for i in range(4)]
        cmul(Wr[0][:, :], Wi[0][:, :], gAr, gAi, inA[0][:, :], inA[1][:, :], tA1[:, :], tA2[:, :])
        cur = 0
        for s in range(L - 1, -1, -1):
            for blk in range(4):
                bdv = BDt[blk][:, :].rearrange("p (g s r e q) -> p g s r e q", g=2, s=L, r=2, e=2)
                for ri in range(2):
                    src = (Wr, Wi)[ri][cur][:, blk * 128:(blk + 1) * 128].rearrange("p (g q) -> p g q", g=2)
                    for e2 in range(2):
                        b.act(bdv[:, :, s, ri, e2, :], src, AF.Copy, scale=maskE[:, e2:e2 + 1])
            if s > 0:
                cmul(Wr[1 - cur][:, :], Wi[1 - cur][:, :], aAr, aAi, Wr[cur][:, :], Wi[cur][:, :], tA1[:, :], tA2[:, :])
                cur = 1 - cur
        for blk in range(4):
            b.dma("sp", s5scr[blk], BDt[blk][:, :], "scr")

        akr = [tsb("akr%d" % i, [128, 32]) for i in range(2)]
        aki = [tsb("aki%d" % i, [128, 32]) for i in range(2)]
        tP1 = tsb("tP1", [128, 32])
        tP2 = tsb("tP2", [128, 32])
        Vre = tsb("Vre", [128, 32, 16])
        Vim = tsb("Vim", [128, 32, 16])
        tV1 = tsb("tV1", [128, 32, 16])
        tV2 = tsb("tV2", [128, 32, 16])
        OUTt = tsb("OUTt", [128, 8, 3072], BF16)
        Cr = inPc[0][:, :].rearrange("p (j h) -> p j h", h=16)
        Ci = inPc[1][:, :].rearrange("p (j h) -> p j h", h=16)
        b.copy("dve", akr[0][:, :], aPr)
        b.copy("dve", aki[0][:, :], aPi)
        cur = 0
        ovv = OUTt[:, :, 1024:3072].rearrange("p g (j s r e h) -> p g j s r e h", j=4, s=L, r=2, e=2)
        for s in range(L):
            bcr = akr[cur][:, :].unsqueeze(2).broadcast_to([128, 32, 16])
            bci = aki[cur][:, :].unsqueeze(2).broadcast_to([128, 32, 16])
            b.tt("dve", tV1[:, :, :], Cr, bcr, ALU.mult)
            b.tt("dve", tV2[:, :, :], Ci, bci, ALU.mult)
            b.tt("dve", Vre[:, :, :], tV1[:, :, :], tV2[:, :, :], ALU.subtract)
            b.tt("dve", tV1[:, :, :], Cr, bci, ALU.mult)
            b.tt("dve", tV2[:, :, :], Ci, bcr, ALU.mult)
            b.tt("dve", Vim[:, :, :], tV1[:, :, :], tV2[:, :, :], ALU.add)
            for ri in range(2):
                src = (Vre, Vim)[ri][:, :, :].rearrange("p (g j) h -> p g j h", g=8)
                for e2 in range(2):
                    msk = maskH[:, 2 * ri + e2:2 * ri + e2 + 1]
                    b.act(ovv[:, :, :, s, ri, e2, :], src, AF.Copy, scale=msk)
            if s == L - 1:
                pass
            cmul(akr[1 - cur][:, :], aki[1 - cur][:, :], aPr, aPi, akr[cur][:, :], aki[cur][:, :], tP1[:, :], tP2[:, :])
            cur = 1 - cur
        aLr, aLi = akr[1 - cur][:, :], aki[1 - cur][:, :]
        b.copy("dve", A1[:, 0, :], aLr)
        b.copy("dve", A1[:, 1, :], aLr)
        b.copy("dve", A2[:, 1, :], aLi)
        b.ts("dve", A2[:, 0, :], aLi, -1.0, ALU.mult)
        b.memset("dve", PW[:, 15, 0, :], 1.0)
        b.memset("dve", PW[:, 15, 1, :], 0.0)
        for k in range(14, -1, -1):
            cmul(PW[:, k, 0, :], PW[:, k, 1, :], aLr, aLi, PW[:, k + 1, 0, :], PW[:, k + 1, 1, :], tP1[:, :], tP2[:, :])
        cmul(A1x[:, 0, :], A2x[:, 1, :], aLr, aLi, PW[:, 0, 0, :], PW[:, 0, 1, :], tP1[:, :], tP2[:, :])
        b.copy("dve", A1x[:, 1, :], A1x[:, 0, :])
        b.ts("dve", A2x[:, 0, :], A2x[:, 1, :], -1.0, ALU.mult)

        aBr = [tsb("aBr%d" % i, [128, 1024]) for i in range(2)]
        aBi = [tsb("aBi%d" % i, [128, 1024]) for i in range(2)]
        tQ1 = tsb("tQ1", [128, 1024])
        tQ2 = tsb("tQ2", [128, 1024])
        nCi = tsb("nCi", [128, 1024])
        q3 = lambda t: t[0:64, :].rearrange("p (g h) -> p g h", h=16)
        bq = lambda a: a.unsqueeze(2).broadcast_to([64, 64, 16])
        cmul(q3(aBr[0]), q3(aBi[0]), bq(gQr), bq(gQi), q3(inQb[0]), q3(inQb[1]), q3(tQ1), q3(tQ2), castn=2)
        b.ts("dve", nCi[0:64, :], inQb[3][0:64, :], -1.0, ALU.mult)
        cur = 0
        for k in range(L):
            for gt in range(KT):
                cs = slice(gt * 128, (gt + 1) * 128)
                reg = ps[gt][:, (k % 4) * 128:(k % 4 + 1) * 128]
                b.mm([(reg, aBr[cur][0:64, cs], inQb[2][0:64, cs], True, False, None),
                      (reg, aBi[cur][0:64, cs], nCi[0:64, cs], False, True, None)])
            if k % 4 == 3:
                for gt in range(KT):
                    o = OUTt[:, gt, (k - 3) * 128:(k + 1) * 128].rearrange("p (k c) -> p k c", k=4)
                    i0 = ps[gt][:, :].rearrange("p (k c) -> p k c", k=4)
                    b.tt("dve", o, i0, blockmask.unsqueeze(1).broadcast_to([128, 4, 128]), ALU.mult)
            if k < L - 1:
                cmul(q3(aBr[1 - cur]), q3(aBi[1 - cur]), bq(aQr), bq(aQi), q3(aBr[cur]), q3(aBi[cur]), q3(tQ1), q3(tQ2), castn=2)
                cur = 1 - cur
        for gt in range(KT):
            b.stt(OUTt[:, gt, 0:128], ident, cvec[:, C_D + gt:C_D + gt + 1], OUTt[:, gt, 0:128], ALU.mult, ALU.add)
            b.dma("sp", s5scr[4 + gt, :, 0:3072], OUTt[:, gt, :], "scr")
        P.seal("scr")
        P.barrier()
    h = [[sb("h%d_%d" % (i, k), [128, T]) for k in range(KT)] for i in range(2)]
    u = [sb("u%d" % k, [128, T], BF16) for k in range(KT)]
    sq = [sb("sq%d" % k, [128, T], BF16) for k in range(KT)]
    z = [sb("z%d" % k, [128, T], BF16) for k in range(KT)]
    q = z
    u5 = [sb("u5_%d" % k, [128, T], BF16) for k in range(KT)]
    sq5 = [sb("sq5_%d" % k, [128, T], BF16) for k in range(KT)]
    Bt = sb("Bt", [128, 2, 32])
    actb = [sb("act%d" % f, [128, T], BF16) for f in range(FT)]
    rstd = sb("rstd", [128, T])
    srt = sb("srt", [128, T])
    rstd5 = sb("rstd5", [128, T])
    srt5 = sb("srt5", [128, T])
    gbuf = [sb("gbuf%d" % i, [128, T + 2]) for i in range(2)]
    cbuf = [sb("cbuf%d" % i, [128, T]) for i in range(2)]
    sbuf_ = [sb("sbuf%d" % i, [128, T]) for i in range(2)]
    pbuf = [sb("pbuf%d" % i, [128, T + 2]) for i in range(2)]
    cgs = [sb("cgs%d" % i, [128, T]) for i in range(2)]
    tmpf = [sb("tmpf%d" % i, [128, T]) for i in range(2)]
    HS = sb("HS", [128, NCH + 1, 2, 32])
    Hb = [sb("Hb%d" % i, [128, 32, NCH], BF16) for i in range(2)]
    rt1 = sb("rt1", [128, 2, 32])
    rt2 = sb("rt2", [128, 2, 32])
    slots = [sb("slot%d" % i, [128, SLOT], BF16) for i in range(NS)]
    b.memset("pool", HS[:, 0, :, :], 0.0)

    state = {"slot": 0, "bank": 0}

    def stream(src, n):
        s = slots[state["slot"]]
        state["slot"] = (state["slot"] + 1) % NS
        b.dma("sp", s[:, 0:n], src, s.name)
        return s

    def bank():
        p = ps[state["bank"]]
        state["bank"] = (state["bank"] + 1) % 8
        return p

    C0 = [0]

    def W(t, lo=0, hi=0):
        return t[:, C0[0] + lo:T + hi]

    def square(hb, k):
        b.act(W(sq[k]), W(hb[k]), AF.Square)

    def rmsnorm(hb, gcol, final=False, presq=False, filler=None, dst=None, sqb=None):
        front = sqb is sq5
        Wn = (lambda t: t[:, :]) if front else W
        sqb = sq if sqb is None else sqb
        if not presq:
            for k in range(KT):
                b.act(Wn(sqb[k]), Wn(hb[k]), AF.Square)
        pb = bank()
        b.mm([(Wn(pb), ones_bf[:, :], Wn(sqb[k]), k == 0, k == KT - 1, None) for k in range(KT)])
        srt_, rstd_ = (srt5, rstd5) if front else (srt, rstd)
        b.act(Wn(srt_), Wn(pb), AF.Sqrt, bias=eps_t[:, :], scale=1.0 / D)
        if filler is not None:
            filler()
        b.recip(Wn(rstd_), Wn(srt_))
        dst = u if dst is None else dst
        for k in range(KT):
            o = Wn(hb[k]) if final else Wn(dst[k])
            b.stt(o, Wn(hb[k]), cvec[:, gcol + k:gcol + k + 1], Wn(rstd_), ALU.mult, ALU.mult)

    HSv = HS[:, :, :, :].rearrange("p c r (g j) -> p c r g j", j=4)
    P.alias["HS"] = ["HS.q0", "HS.q1", "HS.q2", "HS.q3", "HS.c0"]
    Bt4 = sb("Bt4", [128, 4, 2, 32])

    def s5_statein(qt):
        w = stream(s5scr[qt], SLOT)
        wv = w[:, :].rearrange("p (g s r m) -> p g s r m", g=2, s=L, r=2)
        pbs = [bank() for _ in range(4)]
        mms = []
        for g2 in range(2):
            gt = 2 * qt + g2
            uv = u5[gt][:, :].rearrange("p (c s) -> p c s", s=L)
            for ri in range(2):
                c0 = (ri * 2 + g2) * NCH
                for s in range(L):
                    for j in range(4):
                        mms.append((pbs[j][:, c0:c0 + NCH],
                                    wv[32 * j:32 * j + 32, g2, s, ri, :],
                                    uv[32 * j:32 * j + 32, :, s],
                                    s == 0, s == L - 1, (32 * j, 0)))
        b.mm(mms)
        for j in range(4):
            for ri in range(2):
                b.act(HSv[:, 1:NCH + 1, ri, 2 * qt:2 * qt + 2, j],
                      pbs[j][:, ri * 128:(ri + 1) * 128].rearrange("p (g c) -> p c g", g=2), AF.Copy,
                      wk=["HS.q%d" % qt])

    def s5_prefix_reduce_q(qt):
        cast_some(1)
        ps8 = slice(8 * qt, 8 * qt + 8)
        hk = ["HS.q%d" % qt]
        xv_ = lambda ri: HS[:, 1:NCH + 1, ri, ps8].rearrange("p (b k) j -> p b k j", b=4)
        pw_ = lambda ri: PW[:, :, ri, ps8].unsqueeze(1).broadcast_to([128, 4, 16, 8])
        v4 = lambda t: t[:, :].rearrange("p (b k j) -> p b k j", b=4, k=16)
        r4 = lambda t: t[:, :].rearrange("p (b k j) -> p b j k", b=4, k=16)
        t1, t2, t3, t4 = tmpf[0], tmpf[1], cbuf[0], cbuf[1]
        b.tt("dve", v4(t1), pw_(0), xv_(0), ALU.mult, rk=hk + ["PW"])
        b.tt("dve", v4(t2), pw_(1), xv_(1), ALU.mult, rk=hk + ["PW"])
        b.stt(t1[:, :], t2[:, :], -1.0, t1[:, :], ALU.mult, ALU.add)
        b.P.op("dve", lambda e, o=Bt4[:, :, 0, ps8], i=r4(t1): e.tensor_reduce(out=o, in_=i, axis=mybir.AxisListType.X, op=ALU.add),
               reads=[t1.name], writes=["Bt4.q%d" % qt])
        b.tt("pool", v4(t3), pw_(0), xv_(1), ALU.mult, rk=hk + ["PW"])
        b.tt("pool", v4(t4), pw_(1), xv_(0), ALU.mult, rk=hk + ["PW"])
        b.tt("pool", t3[:, :], t3[:, :], t4[:, :], ALU.add)
        b.P.op("dve", lambda e, o=Bt4[:, :, 1, ps8], i=r4(t3): e.tensor_reduce(out=o, in_=i, axis=mybir.AxisListType.X, op=ALU.add),
               reads=[t3.name], writes=["Bt4i.q%d" % qt])

    def s5_prefix_combine():
        bk = ["Bt4.q%d" % q for q in range(4)] + ["Bt4i.q%d" % q for q in range(4)]
        for blk in range(4):
            b.tt("dve", rt1[:, :, :], A1x[:, :, :], HS[:, 0, :, :], ALU.mult, rk=["A1x", "HS.c0"])
            b.tt("dve", rt2[:, 0, :], A2x[:, 0, :], HS[:, 0, 1, :], ALU.mult, rk=["A2x", "HS.c0"])
            b.tt("dve", rt2[:, 1, :], A2x[:, 1, :], HS[:, 0, 0, :], ALU.mult, rk=["A2x", "HS.c0"])
            b.tt("dve", rt1[:, :, :], rt1[:, :, :], rt2[:, :, :], ALU.add)
            b.tt("dve", HS[:, 0, :, :], rt1[:, :, :], Bt4[:, blk, :, :], ALU.add, rk=["rt1"] + bk, wk=["HS.c0"])

    def s5_recur(eng):
        for c in range(1, NCH + 1):
            b.tt(eng, rt1[:, :, :], A1[:, :, :], HS[:, c - 1, :, :], ALU.mult)
            b.tt(eng, rt2[:, 0, :], A2[:, 0, :], HS[:, c - 1, 1, :], ALU.mult)
            b.tt(eng, rt2[:, 1, :], A2[:, 1, :], HS[:, c - 1, 0, :], ALU.mult)
            b.tt(eng, HS[:, c, :, :], HS[:, c, :, :], rt1[:, :, :], ALU.add)
            b.tt(eng, HS[:, c, :, :], HS[:, c, :, :], rt2[:, :, :], ALU.add)

    def s5_carry(eng="pool"):
        b.copy(eng, HS[:, 0, :, :], HS[:, NCH, :, :])

    def s5_hb(eng):
        for ri in range(2):
            b.copy(eng, Hb[ri][:, :, :], HS[:, 0:NCH, ri, :].rearrange("p c j -> p j c"))
        s5_carry(eng)

    def s5_back():
        cc = C0[0] // L
        for gt in range(KT):
            w = stream(s5scr[4 + gt, :, 0:3072], 3072)
            kv = w[:, 0:1024].rearrange("p (k c) -> p k c", k=L)
            vv = w[:, 1024:3072].rearrange("p (j s r m) -> p j s r m", j=4, s=L, r=2)
            pb = bank()
            yv = pb[:, :].rearrange("p (c s) -> p c s", s=L)
            uv = u5[gt][:, :].rearrange("p (c s) -> p c s", s=L)
            mms = [(W(pb), kv[:, 0, :], W(u5[gt]), True, False, None)]
            for j in range(4):
                for s in range(L):
                    for ri in range(2):
                        mms.append((yv[32 * j:32 * j + 32, cc:NCH, s], vv[:, j, s, ri, :],
                                    Hb[ri][:, 4 * gt + j, cc:NCH], False, False, (0, 32 * j)))
            for k in range(1, L):
                if SIM_SAFE:
                    for s in range(k, L):
                        mms.append((yv[:, cc:NCH, s], kv[:, k, :], uv[:, cc:NCH, s - k], False, False, None))
                else:
                    mms.append((yv[:, cc:NCH, k:L], kv[:, k, :], uv[:, cc:NCH, 0:L - k], False, False, None))
            last = mms[-1]
            mms[-1] = (last[0], last[1], last[2], False, True, last[5])
            b.mm(mms)
            b.act(W(z[gt]), W(pb), AF.Gelu_apprx_tanh)

    def glu(hb, hook=None):
        pend = []

        def stage_b(i, t):
            b.tt("pool", W(hb[i]), W(hb[i]), W(t), ALU.add)
            square(hb, i)

        for blk in range(4):
            if blk == 1 and hook is not None:
                hook()
            w = stream(wbf[B_GLU + blk], SLOT)
            wv = w[:, :].rearrange("p (i a k c) -> p i a k c", i=2, a=2, k=KT)
            for i2 in range(2):
                i = 2 * blk + i2
                pa, pg = bank(), bank()
                b.mm([(W(pa), wv[:, i2, 0, k, :], W(z[k]), k == 0, k == KT - 1, None) for k in range(KT)])
                b.mm([(W(pg), wv[:, i2, 1, k, :], W(z[k]), k == 0, k == KT - 1, None) for k in range(KT)])
                s = sbuf_[i % 2]
                b.act(W(s), W(pg), AF.Sigmoid)
                t = tmpf[i % 2]
                b.tt("dve", W(t), W(pa), W(s), ALU.mult)
                if pend:
                    stage_b(*pend.pop())
                pend.append((i, t))
        stage_b(*pend.pop())

    def ffn(hb, layer, par, filler=None, sqr=True):
        b_up = (B_UP0, B_UP1)[layer]
        b_dn = (B_DN0, B_DN1)[layer]
        halo_old, halo_new = haloF[layer][par], haloF[layer][1 - par]
        c0 = C0[0]
        pend = []

        def stage_b(f, pv):
            cb, sg = cbuf[f % 2], sbuf_[f % 2]
            bcol = C_FB + layer * FT + f
            b.act(W(sg), W(cb), AF.Silu, bias=cvec[:, bcol:bcol + 1])
            b.tt("dve", W(actb[f]), W(sg), W(pv), ALU.mult)

        for blk in range(FT // 2):
            w = stream(wbf[b_up + blk], SLOT)
            wv = w[:, :].rearrange("p (f a k c) -> p f a k c", f=2, a=2, k=KT)
            for f2 in range(2):
                f = 2 * blk + f2
                pg, pv = bank(), bank()
                b.mm([(W(pg), wv[:, f2, 0, k, :], W(u[k]), k == 0, k == KT - 1, None) for k in range(KT)])
                b.mm([(W(pv), wv[:, f2, 1, k, :], W(u[k]), k == 0, k == KT - 1, None) for k in range(KT)])
                gb, cb = gbuf[f % 2], cbuf[f % 2]
                wc = lambda kk: cvec[:, C_FW + (layer * 3 + kk) * FT + f: C_FW + (layer * 3 + kk) * FT + f + 1]
                b.act(W(gb, 2, 2), W(pg), AF.Copy)
                b.act(gb[:, c0:c0 + 2], halo_old[:, f, :], AF.Copy)
                b.act(halo_new[:, f, :], pg[:, T - 2:T], AF.Copy)
                b.act(W(cb), W(pg), AF.Copy, scale=wc(2))
                b.stt(W(cb), W(gb, 1, 1), wc(1), W(cb), ALU.mult, ALU.add)
                b.stt(W(cb), W(gb, 0, 0), wc(0), W(cb), ALU.mult, ALU.add)
                if pend:
                    stage_b(*pend.pop())
                pend.append((f, pv))
        stage_b(*pend.pop())
        if filler is not None:
            filler()
        FH = FT // 2
        for hf in range(2):
            for bi in range(4):
                w = stream(wbf[b_dn + hf * 4 + bi, :, 0:2 * FH * 128], 2 * FH * 128)
                wv = w[:, 0:2 * FH * 128].rearrange("p (m f c) -> p m f c", m=2, f=FH)
                for m2 in range(2):
                    m = 2 * bi + m2
                    pb = bank()
                    b.mm([(W(pb), wv[:, m2, f, :], W(actb[hf * FH + f]), f == 0, f == FH - 1, None)
                          for f in range(FH)])
                    b.tt("dve", W(hb[m]), W(hb[m]), W(pb), ALU.add)
                    if sqr and hf == 1:
                        square(hb, m)

    def shortconv(hb, par):
        halo_old, halo_new = haloS[par], haloS[1 - par]
        c0 = C0[0]
        for i in range(KT):
            w = stream(wbf[B_IN + i, :, 0:3072], 3072)
            wv = w[:, 0:3072].rearrange("p (a k c) -> p a k c", a=3, k=KT)
            pbg, pcg, phh = bank(), bank(), bank()
            for a, pp in ((0, pbg), (1, pcg), (2, phh)):
                b.mm([(W(pp), wv[:, a, k, :], W(u[k]), k == 0, k == KT - 1, None) for k in range(KT)])
            cg, pbf, cb = cgs[i % 2], pbuf[i % 2], cbuf[i % 2]
            b.act(W(cg), W(pcg), AF.Copy)
            b.tt("dve", W(pbf, 2, 2), W(cg), W(phh), ALU.mult)
            b.act(pbf[:, c0:c0 + 2], halo_old[:, i, :], AF.Copy)
            b.act(halo_new[:, i, :], pbf[:, T:T + 2], AF.Copy)
            wc = lambda kk: cvec[:, C_SCW + kk * KT + i: C_SCW + kk * KT + i + 1]
            b.ts("dve", W(cb), W(pbf, 2, 2), wc(2), ALU.mult)
            b.stt(W(cb), W(pbf, 1, 1), wc(1), W(cb), ALU.mult, ALU.add)
            b.stt(W(cb), W(pbf, 0, 0), wc(0), W(cb), ALU.mult, ALU.add)
            b.tt("dve", W(q[i]), W(cb), W(pbg), ALU.mult)
        for blk in range(2):
            w = stream(wbf[B_OUT + blk], SLOT)
            wv = w[:, :].rearrange("p (m k c) -> p m k c", m=4, k=KT)
            for m4 in range(4):
                m = 4 * blk + m4
                pb = bank()
                b.mm([(W(pb), wv[:, m4, k, :], W(q[k]), k == 0, k == KT - 1, None) for k in range(KT)])
                b.tt("dve", W(hb[m]), W(hb[m]), W(pb), ALU.add)
                square(hb, m)

    xv = xT.rearrange("(k p) t -> k p t", p=128)
    ov = outT.rearrange("(k p) t -> k p t", p=128)
    def load_x(ti, eng="sp"):
        hb = h[ti % 2]
        return [b.dma(eng, hb[k][:, :], xv[k, :, ti * T:(ti + 1) * T], hb[k].name) for k in range(KT)]

    def front_a(ti):
        rmsnorm(h[ti % 2], C_NM0, dst=u5, sqb=sq5)

    def front_sq(ti):
        for k in range(KT):
            b.act(sq5[k][:, :], h[ti % 2][k][:, :], AF.Square)

    def front_rest(ti):
        rmsnorm(h[ti % 2], C_NM0, dst=u5, sqb=sq5, presq=True)

    for ti in range(npre + 1):
        evs = load_x(ti)
        if ti == npre:
            cast_some(NBLK)
            for g in range(7):
                P.seal("cast%d" % g)
        front_a(ti)
        for qt in range(4):
            s5_statein(qt)
            if ti < npre:
                s5_prefix_reduce_q(qt)
        if ti < npre:
            s5_prefix_combine()
        else:
            s5_recur("dve")
            s5_hb("dve")
    for ti in range(npre, ntile):
        hb = h[ti % 2]
        par = (ti - npre) % 2
        nxt = ti + 1 < ntile
        C0[0] = (T - WARM_COLS) if ti == npre else 0
        if ti == npre and nxt:
            s5_back()
            load_x(ti + 1)
            front_a(ti + 1)
            for qt in range(4):
                s5_statein(qt)
            s5_recur("pool")
            glu(hb)
            rmsnorm(hb, C_NF0, presq=True)
            ffn(hb, 0, par)
            rmsnorm(hb, C_NM1, presq=True)
        else:
            if nxt:
                load_x(ti + 1, "pool")
            s5_back()
            if nxt:
                front_sq(ti + 1)
            glu(hb, hook=(lambda: front_rest(ti + 1)) if nxt else None)
            rmsnorm(hb, C_NF0, presq=True, filler=(lambda: (s5_statein(0), s5_statein(1))) if nxt else None)
            ffn(hb, 0, par)
            rmsnorm(hb, C_NM1, presq=True, filler=(lambda: (s5_statein(2), s5_statein(3))) if nxt else None)
            if nxt:
                s5_recur("pool")
        shortconv(hb, par)
        rmsnorm(hb, C_NF1, presq=True)
        ffn(hb, 1, par, filler=(lambda: s5_hb("act")) if nxt else None)
        if ti > npre:
            mi = ti - npre - 1
            rmsnorm(hb, C_NFIN, final=True, presq=True)
            for k in range(KT):
                b.dma("pool", ov[k, :, mi * T:(mi + 1) * T], hb[k][:, :], "o" + hb[k].name)
    P.emit()
    return nc


def _cols(v):
    return np.ascontiguousarray(v.reshape(-1, 128).T)


def prepare_shared(inp):
    f32 = np.float32
    wst = np.zeros((NBLK, 128, SLOT), f32)
    wg = inp["s5_w_glu"][0]
    for blk in range(4):
        v = wst[B_GLU + blk].reshape(128, 2, 2, KT, 128)
        for i2 in range(2):
            i = 2 * blk + i2
            for a in range(2):
                c0 = a * D + i * 128
                v[:, i2, a] = wg[:, c0:c0 + 128].reshape(KT, 128, 128).transpose(1, 0, 2)
    for layer in range(2):
        wu = inp["ffn_w_up"][layer]
        wd = inp["ffn_w_down"][layer]
        b_up = (B_UP0, B_UP1)[layer]
        b_dn = (B_DN0, B_DN1)[layer]
        for blk in range(FT // 2):
            v = wst[b_up + blk].reshape(128, 2, 2, KT, 128)
            for f2 in range(2):
                f = 2 * blk + f2
                for a in range(2):
                    c0 = a * FF + f * 128
                    v[:, f2, a] = wu[:, c0:c0 + 128].reshape(KT, 128, 128).transpose(1, 0, 2)
        FH = FT // 2
        for hf in range(2):
            for bi in range(4):
                v = wst[b_dn + hf * 4 + bi][:, 0:2 * FH * 128].reshape(128, 2, FH, 128)
                for m2 in range(2):
                    m = 2 * bi + m2
                    v[:, m2] = wd[hf * FH * 128:(hf + 1) * FH * 128, m * 128:(m + 1) * 128].reshape(FH, 128, 128).transpose(1, 0, 2)
    wi = inp["sc_w_in"][0]
    wo = inp["sc_w_out"][0]
    for i in range(KT):
        v = wst[B_IN + i][:, 0:3072].reshape(128, 3, KT, 128)
        for a in range(3):
            c0 = a * D + i * 128
            v[:, a] = wi[:, c0:c0 + 128].reshape(KT, 128, 128).transpose(1, 0, 2)
    for blk in range(2):
        v = wst[B_OUT + blk].reshape(128, 4, KT, 128)
        for m4 in range(4):
            m = 4 * blk + m4
            v[:, m4] = wo[:, m * 128:(m + 1) * 128].reshape(KT, 128, 128).transpose(1, 0, 2)

    cvec = np.zeros((128, NCV), f32)
    cvec[:, C_NM0:C_NM0 + 8] = _cols(inp["norm_mix"][0])
    cvec[:, C_NM1:C_NM1 + 8] = _cols(inp["norm_mix"][1])
    cvec[:, C_NF0:C_NF0 + 8] = _cols(inp["norm_ffn"][0])
    cvec[:, C_NF1:C_NF1 + 8] = _cols(inp["norm_ffn"][1])
    cvec[:, C_NFIN:C_NFIN + 8] = _cols(inp["norm_final"])
    cvec[:, C_D:C_D + 8] = _cols(inp["s5_d"][0])
    for kk in range(3):
        cvec[:, C_SCW + kk * KT:C_SCW + (kk + 1) * KT] = _cols(inp["sc_conv_w"][0, kk])
    for layer in range(2):
        for kk in range(3):
            c0 = C_FW + (layer * 3 + kk) * FT
            cvec[:, c0:c0 + FT] = _cols(inp["ffn_conv_w"][layer, kk])
        c0 = C_FB + layer * FT
        cvec[:, c0:c0 + FT] = _cols(inp["ffn_conv_b"][layer])

    cst = np.zeros((128, 264), f32)
    pidx = np.arange(128)
    cst[:, 0:128] = (pidx[:, None] // 16 == pidx[None, :] // 16)
    cst[:, 128:256] = np.eye(128)
    cst[:, 256] = ((pidx // 16) % 2 == 0)
    cst[:, 257] = ((pidx // 16) % 2 == 1)
    cst[:, 258] = (pidx < 64)
    cst[:, 259] = (pidx >= 64)
    cst[:, 260] = -1.0 * (pidx < 64)
    cst[:, 261] = -1.0 * (pidx >= 64)

    a_re, a_im, ldt = inp["s5_a_re"][0], inp["s5_a_im"][0], inp["s5_log_dt"][0]
    b_re, b_im = inp["s5_b_re"][0], inp["s5_b_im"][0]
    c_re, c_im = inp["s5_c_re"][0], inp["s5_c_im"][0]

    def layA_b(x):
        return np.ascontiguousarray(x.reshape(8, 8, 64, 16).transpose(1, 3, 0, 2).reshape(128, 512))

    def layA_a(x):
        y = x.reshape(8, 8, 64).transpose(1, 0, 2)
        return np.ascontiguousarray(np.broadcast_to(y[:, None], (8, 16, 8, 64)).reshape(128, 512))

    ldt_gp = np.broadcast_to(ldt[:, None], (64, 64))
    s5A = np.stack([layA_b(b_re), layA_b(b_im), layA_a(a_re), layA_a(a_im), layA_a(ldt_gp)]).astype(f32)

    def layP_a(x):
        return np.ascontiguousarray(x.reshape(32, 2, 64).transpose(1, 2, 0).reshape(128, 32))

    def layP_c(x):
        return np.ascontiguousarray(x.reshape(32, 2, 16, 64).transpose(1, 3, 0, 2).reshape(128, 512))

    s5Ps = np.stack([layP_a(a_re), layP_a(a_im), layP_a(ldt_gp)]).astype(f32)
    s5Pc = np.stack([layP_c(c_re), layP_c(c_im)]).astype(f32)
    s5Qs = np.stack([a_re.T, a_im.T, ldt_gp.T]).astype(f32)
    s5Qb = np.stack([b_re.transpose(1, 0, 2).reshape(64, 1024), b_im.transpose(1, 0, 2).reshape(64, 1024),
                     c_re.transpose(2, 0, 1).reshape(64, 1024), c_im.transpose(2, 0, 1).reshape(64, 1024)]).astype(f32)
    return {"wst": wst, "cvec": cvec, "cst": cst, "s5A": np.ascontiguousarray(s5A), "s5Ps": np.ascontiguousarray(s5Ps),
            "s5Pc": np.ascontiguousarray(s5Pc), "s5Qs": np.ascontiguousarray(s5Qs), "s5Qb": np.ascontiguousarray(s5Qb)}


NPRE = 7
NMAIN = 8
SIM_SAFE = False
WARM_COLS = 16
STOP = None


def kernel(**inputs):
    inp = {k: np.asarray(v) for k, v in inputs.items()}
    x = inp["x"].astype(np.float32, copy=False)
    bsz, seq, d = x.shape
    half = seq // 2
    shared = prepare_shared(inp)
    in_maps = []
    for c in range(8):
        bi, hf = c // 2, c % 2
        xt = np.zeros((D, 2 * half), np.float32)
        if hf == 1:
            xt[:, 0:half] = x[bi, 0:half].T
        xt[:, half:] = x[bi, hf * half:(hf + 1) * half].T
        m = dict(shared)
        m["xT"] = xt
        in_maps.append(m)
    nc = build_program(NPRE, NMAIN)
    res = run_bass_kernel_spmd(nc, in_maps, core_ids=list(range(8)))
    out = np.empty((bsz, seq, d), np.float32)
    for c in range(8):
        bi, hf = c // 2, c % 2
        out[bi, hf * half:(hf + 1) * half] = res.results[c]["outT"].T
    return out
```

```python
import contextlib
import numpy as np
import concourse.bass as bass
import concourse.mybir as mybir
from concourse.bass_utils import run_bass_kernel_spmd

F32 = mybir.dt.float32
BF16 = mybir.dt.bfloat16
AF = mybir.ActivationFunctionType
ALU = mybir.AluOpType

D = 1024
KT = 8
FF = 2816
FT = 22
T = 512
L = 8
NCH = T // L
NS = 5
SLOT = 4096
RMS_EPS = 1e-6
ENGS = ("pe", "act", "dve", "pool", "sp")

B_GLU, B_UP0, B_DN0, B_IN, B_OUT, B_UP1, B_DN1, NBLK = 0, 4, 15, 23, 31, 33, 44, 52
C_NM0, C_NM1, C_NF0, C_NF1, C_NFIN, C_D, C_SCW, C_FW, C_FB, NCV = 0, 8, 16, 24, 32, 40, 48, 72, 204, 248


class Prog:
    def __init__(self, nc):
        self.nc = nc
        self.cnt = {e: 0 for e in ENGS}
        self.ops = {e: [] for e in ENGS}
        self.writer = {}
        self.readers = {}
        self.seen = {e: {} for e in ENGS}
        self.pending = {e: {} for e in ENGS}
        self.chan_cnt = {}
        self.chan_events = {}
        self.sems = {}
        self.alias = {}

    def _need(self, eng, key, val, waits):
        if key == eng and eng == "pe":
            return
        if self.seen[eng].get(key, 0) >= val:
            return
        waits[key] = max(waits.get(key, 0), val)

    def op(self, eng, fn, reads=(), writes=(), chan=None, after=()):
        reads = [k for n in reads for k in self.alias.get(n, (n,))]
        writes = [k for n in writes for k in self.alias.get(n, (n,))]
        waits = {}
        for ev in after:
            self._need(eng, ev[0], ev[1], waits)
        for k, v in self.pending[eng].items():
            self._need(eng, k, v, waits)
        self.pending[eng] = {}
        for b in reads:
            ev = self.writer.get(b)
            if ev is not None:
                self._need(eng, ev[0], ev[1], waits)
        for b in writes:
            ev = self.writer.get(b)
            if ev is not None:
                self._need(eng, ev[0], ev[1], waits)
            for ev in self.readers.get(b, ()):
                self._need(eng, ev[0], ev[1], waits)
        for k, v in waits.items():
            self.seen[eng][k] = v
        if chan is None:
            self.cnt[eng] += 1
            ev = [eng, self.cnt[eng]]
        else:
            key = "c:" + chan
            self.chan_cnt[key] = self.chan_cnt.get(key, 0) + 16
            ev = [key, self.chan_cnt[key]]
            self.chan_events.setdefault(key, []).append(ev)
        for b in writes:
            self.writer[b] = ev
            self.readers[b] = []
        for b in reads:
            if b not in writes:
                self.readers.setdefault(b, []).append(ev)
        self.ops[eng].append((waits, fn, ev))
        return ev

    def seal(self, chan):
        key = "c:" + chan
        tot = self.chan_cnt.get(key, 0)
        for ev in self.chan_events.get(key, []):
            ev[1] = tot

    def barrier(self):
        snap = dict(self.cnt)
        snap.update({k: v for k, v in self.chan_cnt.items() if not k.startswith("c:cast")})
        for e in ENGS:
            p = self.pending[e]
            for k, v in snap.items():
                if v > 0 and k != e:
                    p[k] = max(p.get(k, 0), v)

    def emit(self):
        nc = self.nc
        keys = sorted(set(ENGS) | set(self.chan_cnt.keys()))
        with contextlib.ExitStack() as st:
            for i, k in enumerate(keys):
                self.sems[k] = st.enter_context(nc.semaphore("s%d" % i))
            fin = {e: self.cnt[e] for e in ENGS if e != "sp" and self.cnt[e] > 0}
            fin.update(self.chan_cnt)
            with nc.Block() as block:
                def run(eng_name, eng):
                    for waits, fn, ev in self.ops[eng_name]:
                        for k, v in waits.items():
                            eng.wait_ge(self.sems[k], v)
                        ins = fn(eng)
                        ins.then_inc(self.sems[ev[0]], 16 if ev[0].startswith("c:") else 1)
                    if eng_name == "sp":
                        for k, v in fin.items():
                            eng.wait_ge(self.sems[k], v)

                @block.tensor
                def _(eng):
                    run("pe", eng)

                @block.scalar
                def _(eng):
                    run("act", eng)

                @block.vector
                def _(eng):
                    run("dve", eng)

                @block.gpsimd
                def _(eng):
                    run("pool", eng)

                @block.sync
                def _(eng):
                    run("sp", eng)


def _nm(*aps):
    return [a.name for a in aps if hasattr(a, "name") and not isinstance(a, (int, float))]


class B:
    def __init__(self, P):
        self.P = P

    def tt(self, eng, out, in0, in1, op, rk=None, wk=None):
        self.P.op(eng, lambda e: e.tensor_tensor(out=out, in0=in0, in1=in1, op=op),
                  reads=_nm(in0, in1) if rk is None else rk, writes=_nm(out) if wk is None else wk)

    def ts(self, eng, out, in0, s1, op0, s2=None, op1=None):
        if op1 is None:
            s2, op1 = 0.0, ALU.add
        fn = lambda e: e.tensor_scalar(out=out, in0=in0, scalar1=s1, scalar2=s2, op0=op0, op1=op1)
        self.P.op(eng, fn, reads=_nm(in0, s1, s2), writes=_nm(out))

    def stt(self, out, in0, scalar, in1, op0, op1):
        self.P.op("dve", lambda e: e.scalar_tensor_tensor(out=out, in0=in0, scalar=scalar, in1=in1, op0=op0, op1=op1),
                  reads=_nm(in0, scalar, in1), writes=_nm(out))

    def act(self, out, in_, func, bias=None, scale=None, wk=None):
        kw = {}
        if bias is not None:
            kw["bias"] = bias
        if scale is not None:
            kw["scale"] = scale
        self.P.op("act", lambda e: e.activation(out=out, in_=in_, func=func, **kw),
                  reads=_nm(in_, bias, scale), writes=_nm(out) if wk is None else wk)

    def copy(self, eng, out, in_):
        if eng == "act":
            return self.act(out, in_, AF.Copy)
        self.P.op(eng, lambda e: e.tensor_copy(out=out, in_=in_), reads=_nm(in_), writes=_nm(out))

    def recip(self, out, in_):
        self.P.op("dve", lambda e: e.reciprocal(out=out, in_=in_), reads=_nm(in_), writes=_nm(out))

    def memset(self, eng, ap, val):
        self.P.op(eng, lambda e: e.memset(ap, val), writes=_nm(ap))

    def dma(self, eng, out, in_, chan, after=()):
        return self.P.op(eng, lambda e: e.dma_start(out=out, in_=in_), reads=_nm(in_), writes=_nm(out), chan=chan,
                         after=after)

    def mm(self, mms):
        reads, writes = [], []
        for m in mms:
            writes += _nm(m[0])
            reads += _nm(m[1], m[2])

        def fn(e):
            ins = None
            for (out, lhsT, rhs, start, stop, tp) in mms:
                if tp is None:
                    ins = e.matmul(out, lhsT, rhs, start=start, stop=stop)
                else:
                    ins = e.matmul(out, lhsT, rhs, start=start, stop=stop, tile_position=tp)
            return ins
        self.P.op("pe", fn, reads=sorted(set(reads)), writes=sorted(set(writes)))


def build_program(npre, nmain):
    ntile = npre + 1 + nmain
    ntok = ntile * T
    nc = bass.Bass("TRN2", target_bir_lowering=False)
    dt_in = lambda name, shape: nc.dram_tensor(name, shape, F32, kind="ExternalInput").ap()
    xT = dt_in("xT", [D, ntok])
    wst = dt_in("wst", [NBLK, 128, SLOT])
    cvec_d = dt_in("cvec", [128, NCV])
    cst_d = dt_in("cst", [128, 264])
    s5A_d = dt_in("s5A", [5, 128, 512])
    s5Ps_d = dt_in("s5Ps", [3, 128, 32])
    s5Pc_d = dt_in("s5Pc", [2, 128, 512])
    s5Qs_d = dt_in("s5Qs", [3, 64, 64])
    s5Qb_d = dt_in("s5Qb", [4, 64, 1024])
    outT = nc.dram_tensor("outT", [D, nmain * T], F32, kind="ExternalOutput").ap()
    GROUPS = [B_GLU, B_UP0, B_DN0, B_IN, B_OUT, B_UP1, B_DN1, NBLK]
    wbf_g = [nc.dram_tensor("wbf%d" % g, [GROUPS[g + 1] - GROUPS[g], 128, SLOT], BF16, kind="Internal").ap()
             for g in range(7)]

    class _WBF:
        def __getitem__(self, key):
            j = key[0] if isinstance(key, tuple) else key
            g = max(i for i in range(7) if GROUPS[i] <= j)
            blk = wbf_g[g][j - GROUPS[g]]
            return blk[key[1:]] if isinstance(key, tuple) else blk
    wbf = _WBF()
    grp_of = lambda j: max(i for i in range(7) if GROUPS[i] <= j)
    s5scr = nc.dram_tensor("s5scr", [12, 128, SLOT], BF16, kind="Internal").ap()

    P = Prog(nc)
    b = B(P)
    sb = lambda name, shape, dt=F32: nc.alloc_sbuf_tensor(name, shape, dt)

    cvec = sb("cvec_s", [128, NCV])
    cst = sb("cst_s", [128, 264])
    ones_bf = sb("ones_bf", [128, 128], BF16)
    eps_t = sb("eps_t", [128, 1])
    hpi_t = sb("hpi_t", [128, 1])
    A1 = sb("A1", [128, 2, 32])
    A2 = sb("A2", [128, 2, 32])
    PW = sb("PW", [128, 16, 2, 32])
    A1x = sb("A1x", [128, 2, 32])
    A2x = sb("A2x", [128, 2, 32])
    haloF = [[sb("haloF%d_%d" % (l, i), [128, FT, 2]) for i in range(2)] for l in range(2)]
    haloS = [sb("haloS%d" % i, [128, KT, 2]) for i in range(2)]
    ps = [nc.alloc_psum_tensor("ps%d" % i, [128, 512], F32) for i in range(8)]

    blockmask = cst[:, 0:128]
    ident = cst[:, 128:256]
    maskE = cst[:, 256:258]
    maskH = cst[:, 258:262]

    b.dma("sp", cvec[:, :], cvec_d, "const")
    b.dma("sp", cst[:, :], cst_d, "const")
    b.memset("pool", ones_bf[:, :], 1.0)
    b.memset("pool", eps_t[:, :], RMS_EPS)
    b.memset("pool", hpi_t[:, :], float(np.pi / 2))
    for l in range(2):
        for i in range(2):
            b.memset("pool", haloF[l][i][:, :, :], 0.0)
    for i in range(2):
        b.memset("pool", haloS[i][:, :, :], 0.0)
    cast_state = {"next": 0}

    def cast_some(n, after=()):
        for _ in range(n):
            j = cast_state["next"]
            if j >= NBLK:
                return
            b.dma("pool", wbf[j], wst[j], "cast%d" % grp_of(j), after=after)
            cast_state["next"] = j + 1

    with contextlib.ExitStack() as tmp:
        def tsb(name, shape, dt=F32):
            return tmp.enter_context(nc.sbuf_tensor(name, shape, dt))

        def cmul(o_r, o_i, a_r, a_i, b_r, b_i, t1, t2, eng="dve", castn=0):
            cast_some(castn)
            b.tt(eng, t1, a_r, b_r, ALU.mult)
            b.tt(eng, t2, a_i, b_i, ALU.mult)
            b.tt(eng, o_r, t1, t2, ALU.subtract)
            b.tt(eng, t1, a_r, b_i, ALU.mult)
            b.tt(eng, t2, a_i, b_r, ALU.mult)
            b.tt(eng, o_i, t1, t2, ALU.add)

        def make_a(pfx, lr, li, ldt, pp, n, want_g, eng="dve"):
            mk = lambda s: tsb(pfx + s, [128, n])[0:pp, :]
            dtt, xr, xi, zr, zi, t1, t2, t3 = [mk(s) for s in ("dt", "xr", "xi", "zr", "zi", "t1", "t2", "t3")]
            b.act(dtt, ldt, AF.Exp)
            b.tt(eng, xr, lr, dtt, ALU.mult)
            b.tt(eng, xi, li, dtt, ALU.mult)
            b.act(t1, xr, AF.Exp, scale=1.0 / 16)
            b.act(t2, xi, AF.Sin, scale=1.0 / 16)
            b.act(t3, xi, AF.Sin, scale=1.0 / 16, bias=hpi_t[0:pp, :])
            b.tt(eng, zr, t1, t3, ALU.mult)
            b.tt(eng, zi, t1, t2, ALU.mult)
            for _ in range(4):
                b.tt(eng, t1, zr, zr, ALU.mult)
                b.tt(eng, t2, zi, zi, ALU.mult)
                b.tt(eng, t3, zr, zi, ALU.mult)
                b.tt(eng, zr, t1, t2, ALU.subtract)
                b.tt(eng, zi, t3, t3, ALU.add)
            if not want_g:
                return zr, zi, None, None
            gr, gi, nr, rden = [mk(s) for s in ("gr", "gi", "nr", "rden")]
            b.tt(eng, t1, lr, lr, ALU.mult)
            b.tt(eng, t2, li, li, ALU.mult)
            b.tt(eng, t1, t1, t2, ALU.add)
            b.recip(rden, t1)
            b.ts(eng, nr, zr, -1.0, ALU.add)
            b.tt(eng, t1, nr, lr, ALU.mult)
            b.tt(eng, t2, zi, li, ALU.mult)
            b.tt(eng, t1, t1, t2, ALU.add)
            b.tt(eng, gr, t1, rden, ALU.mult)
            b.tt(eng, t1, zi, lr, ALU.mult)
            b.tt(eng, t2, nr, li, ALU.mult)
            b.tt(eng, t1, t1, t2, ALU.subtract)
            b.tt(eng, gi, t1, rden, ALU.mult)
            return zr, zi, gr, gi

        inA = [tsb("inA%d" % i, [128, 512]) for i in range(5)]
        for i in range(5):
            b.dma("sp", inA[i][:, :], s5A_d[i], "const")
        inPs = [tsb("inPs%d" % i, [128, 32]) for i in range(3)]
        for i in range(3):
            b.dma("sp", inPs[i][:, :], s5Ps_d[i], "const")
        inPc = [tsb("inPc%d" % i, [128, 512]) for i in range(2)]
        for i in range(2):
            b.dma("sp", inPc[i][:, :], s5Pc_d[i], "const")
        inQs = [tsb("inQs%d" % i, [128, 64]) for i in range(3)]
        for i in range(3):
            b.dma("sp", inQs[i][0:64, :], s5Qs_d[i], "const")
        inQb = [tsb("inQb%d" % i, [128, 1024]) for i in range(4)]
        for i in range(4):
            b.dma("sp", inQb[i][0:64, :], s5Qb_d[i], "const")
        P.seal("const")

        aAr, aAi, gAr, gAi = make_a("A_", inA[2][:, :], inA[3][:, :], inA[4][:, :], 128, 512, True)
        aPr, aPi, _, _ = make_a("P_", inPs[0][:, :], inPs[1][:, :], inPs[2][:, :], 128, 32, False)
        aQr, aQi, gQr, gQi = make_a("Q_", inQs[0][0:64, :], inQs[1][0:64, :], inQs[2][0:64, :], 64, 64, True)
        cast_some(14)
        Wr = [tsb("WrA%d" % i, [128, 512]) for i in range(2)]
        Wi = [tsb("WiA%d" % i, [128, 512]) for i in range(2)]
        tA1 = tsb("tA1", [128, 512])
        tA2 = tsb("tA2", [128, 512])
        BDt = [tsb("BDt%d" % i, [128, SLOT], BF16) for i in range(4)]
        cmul(Wr[0][:, :], Wi[0][:, :], gAr, gAi, inA[0][:, :], inA[1][:, :], tA1[:, :], tA2[:, :])
        cur = 0
        for s in range(L - 1, -1, -1):
            for blk in range(4):
                bdv = BDt[blk][:, :].rearrange("p (g s r e q) -> p g s r e q", g=2, s=L, r=2, e=2)
                for ri in range(2):
                    src = (Wr, Wi)[ri][cur][:, blk * 128:(blk + 1) * 128].rearrange("p (g q) -> p g q", g=2)
                    for e2 in range(2):
                        b.act(bdv[:, :, s, ri, e2, :], src, AF.Copy, scale=maskE[:, e2:e2 + 1])
            if s > 0:
                cmul(Wr[1 - cur][:, :], Wi[1 - cur][:, :], aAr, aAi, Wr[cur][:, :], Wi[cur][:, :], tA1[:, :], tA2[:, :])
                cur = 1 - cur
        for blk in range(4):
            b.dma("sp", s5scr[blk], BDt[blk][:, :], "scr")

        akr = [tsb("akr%d" % i, [128, 32]) for i in range(2)]
        aki = [tsb("aki%d" % i, [128, 32]) for i in range(2)]
        tP1 = tsb("tP1", [128, 32])
        tP2 = tsb("tP2", [128, 32])
        Vre = tsb("Vre", [128, 32, 16])
        Vim = tsb("Vim", [128, 32, 16])
        tV1 = tsb("tV1", [128, 32, 16])
        tV2 = tsb("tV2", [128, 32, 16])
        OUTt = tsb("OUTt", [128, 8, 3072], BF16)
        Cr = inPc[0][:, :].rearrange("p (j h) -> p j h", h=16)
        Ci = inPc[1][:, :].rearrange("p (j h) -> p j h", h=16)
        b.copy("dve", akr[0][:, :], aPr)
        b.copy("dve", aki[0][:, :], aPi)
        cur = 0
        ovv = OUTt[:, :, 1024:3072].rearrange("p g (j s r e h) -> p g j s r e h", j=4, s=L, r=2, e=2)
        for s in range(L):
            bcr = akr[cur][:, :].unsqueeze(2).broadcast_to([128, 32, 16])
            bci = aki[cur][:, :].unsqueeze(2).broadcast_to([128, 32, 16])
            b.tt("dve", tV1[:, :, :], Cr, bcr, ALU.mult)
            b.tt("dve", tV2[:, :, :], Ci, bci, ALU.mult)
            b.tt("dve", Vre[:, :, :], tV1[:, :, :], tV2[:, :, :], ALU.subtract)
            b.tt("dve", tV1[:, :, :], Cr, bci, ALU.mult)
            b.tt("dve", tV2[:, :, :], Ci, bcr, ALU.mult)
            b.tt("dve", Vim[:, :, :], tV1[:, :, :], tV2[:, :, :], ALU.add)
            for ri in range(2):
                src = (Vre, Vim)[ri][:, :, :].rearrange("p (g j) h -> p g j h", g=8)
                for e2 in range(2):
                    msk = maskH[:, 2 * ri + e2:2 * ri + e2 + 1]
                    b.act(ovv[:, :, :, s, ri, e2, :], src, AF.Copy, scale=msk)
            if s == L - 1:
                pass
            cmul(akr[1 - cur][:, :], aki[1 - cur][:, :], aPr, aPi, akr[cur][:, :], aki[cur][:, :], tP1[:, :], tP2[:, :])
            cur = 1 - cur
        aLr, aLi = akr[1 - cur][:, :], aki[1 - cur][:, :]
        b.copy("dve", A1[:, 0, :], aLr)
        b.copy("dve", A1[:, 1, :], aLr)
        b.copy("dve", A2[:, 1, :], aLi)
        b.ts("dve", A2[:, 0, :], aLi, -1.0, ALU.mult)
        b.memset("dve", PW[:, 15, 0, :], 1.0)
        b.memset("dve", PW[:, 15, 1, :], 0.0)
        for k in range(14, -1, -1):
            cmul(PW[:, k, 0, :], PW[:, k, 1, :], aLr, aLi, PW[:, k + 1, 0, :], PW[:, k + 1, 1, :], tP1[:, :], tP2[:, :])
        cmul(A1x[:, 0, :], A2x[:, 1, :], aLr, aLi, PW[:, 0, 0, :], PW[:, 0, 1, :], tP1[:, :], tP2[:, :])
        b.copy("dve", A1x[:, 1, :], A1x[:, 0, :])
        b.ts("dve", A2x[:, 0, :], A2x[:, 1, :], -1.0, ALU.mult)

        aBr = [tsb("aBr%d" % i, [128, 1024]) for i in range(2)]
        aBi = [tsb("aBi%d" % i, [128, 1024]) for i in range(2)]
        tQ1 = tsb("tQ1", [128, 1024])
        tQ2 = tsb("tQ2", [128, 1024])
        nCi = tsb("nCi", [128, 1024])
        q3 = lambda t: t[0:64, :].rearrange("p (g h) -> p g h", h=16)
        bq = lambda a: a.unsqueeze(2).broadcast_to([64, 64, 16])
        cmul(q3(aBr[0]), q3(aBi[0]), bq(gQr), bq(gQi), q3(inQb[0]), q3(inQb[1]), q3(tQ1), q3(tQ2), castn=2)
        b.ts("dve", nCi[0:64, :], inQb[3][0:64, :], -1.0, ALU.mult)
        cur = 0
        for k in range(L):
            for gt in range(KT):
                cs = slice(gt * 128, (gt + 1) * 128)
                reg = ps[gt][:, (k % 4) * 128:(k % 4 + 1) * 128]
                b.mm([(reg, aBr[cur][0:64, cs], inQb[2][0:64, cs], True, False, None),
                      (reg, aBi[cur][0:64, cs], nCi[0:64, cs], False, True, None)])
            if k % 4 == 3:
                for gt in range(KT):
                    o = OUTt[:, gt, (k - 3) * 128:(k + 1) * 128].rearrange("p (k c) -> p k c", k=4)
                    i0 = ps[gt][:, :].rearrange("p (k c) -> p k c", k=4)
                    b.tt("dve", o, i0, blockmask.unsqueeze(1).broadcast_to([128, 4, 128]), ALU.mult)
            if k < L - 1:
                cmul(q3(aBr[1 - cur]), q3(aBi[1 - cur]), bq(aQr), bq(aQi), q3(aBr[cur]), q3(aBi[cur]), q3(tQ1), q3(tQ2), castn=2)
                cur = 1 - cur
        for gt in range(KT):
            b.stt(OUTt[:, gt, 0:128], ident, cvec[:, C_D + gt:C_D + gt + 1], OUTt[:, gt, 0:128], ALU.mult, ALU.add)
            b.dma("sp", s5scr[4 + gt, :, 0:3072], OUTt[:, gt, :], "scr")
        P.seal("scr")
        P.barrier()
    h = [[sb("h%d_%d" % (i, k), [128, T]) for k in range(KT)] for i in range(2)]
    u = [sb("u%d" % k, [128, T], BF16) for k in range(KT)]
    sq = [sb("sq%d" % k, [128, T], BF16) for k in range(KT)]
    z = [sb("z%d" % k, [128, T], BF16) for k in range(KT)]
    q = z
    u5 = [sb("u5_%d" % k, [128, T], BF16) for k in range(KT)]
    sq5 = [sb("sq5_%d" % k, [128, T], BF16) for k in range(KT)]
    Bt = sb("Bt", [128, 2, 32])
    actb = [sb("act%d" % f, [128, T], BF16) for f in range(FT)]
    rstd = sb("rstd", [128, T])
    srt = sb("srt", [128, T])
    rstd5 = sb("rstd5", [128, T])
    srt5 = sb("srt5", [128, T])
    gbuf = [sb("gbuf%d" % i, [128, T + 2]) for i in range(2)]
    cbuf = [sb("cbuf%d" % i, [128, T]) for i in range(2)]
    sbuf_ = [sb("sbuf%d" % i, [128, T]) for i in range(2)]
    pbuf = [sb("pbuf%d" % i, [128, T + 2]) for i in range(2)]
    cgs = [sb("cgs%d" % i, [128, T]) for i in range(2)]
    tmpf = [sb("tmpf%d" % i, [128, T]) for i in range(2)]
    HS = sb("HS", [128, NCH + 1, 2, 32])
    Hb = [sb("Hb%d" % i, [128, 32, NCH], BF16) for i in range(2)]
    rt1 = sb("rt1", [128, 2, 32])
    rt2 = sb("rt2", [128, 2, 32])
    slots = [sb("slot%d" % i, [128, SLOT], BF16) for i in range(NS)]
    b.memset("pool", HS[:, 0, :, :], 0.0)

    state = {"slot": 0, "bank": 0}

    def stream(src, n):
        s = slots[state["slot"]]
        state["slot"] = (state["slot"] + 1) % NS
        b.dma("sp", s[:, 0:n], src, s.name)
        return s

    def bank():
        p = ps[state["bank"]]
        state["bank"] = (state["bank"] + 1) % 8
        return p

    C0 = [0]

    def W(t, lo=0, hi=0):
        return t[:, C0[0] + lo:T + hi]

    def square(hb, k):
        b.act(W(sq[k]), W(hb[k]), AF.Square)

    def rmsnorm(hb, gcol, final=False, presq=False, filler=None, dst=None, sqb=None):
        front = sqb is sq5
        Wn = (lambda t: t[:, :]) if front else W
        sqb = sq if sqb is None else sqb
        if not presq:
            for k in range(KT):
                b.act(Wn(sqb[k]), Wn(hb[k]), AF.Square)
        pb = bank()
        b.mm([(Wn(pb), ones_bf[:, :], Wn(sqb[k]), k == 0, k == KT - 1, None) for k in range(KT)])
        srt_, rstd_ = (srt5, rstd5) if front else (srt, rstd)
        b.act(Wn(srt_), Wn(pb), AF.Sqrt, bias=eps_t[:, :], scale=1.0 / D)
        if filler is not None:
            filler()
        b.recip(Wn(rstd_), Wn(srt_))
        dst = u if dst is None else dst
        for k in range(KT):
            o = Wn(hb[k]) if final else Wn(dst[k])
            b.stt(o, Wn(hb[k]), cvec[:, gcol + k:gcol + k + 1], Wn(rstd_), ALU.mult, ALU.mult)

    HSv = HS[:, :, :, :].rearrange("p c r (g j) -> p c r g j", j=4)
    P.alias["HS"] = ["HS.q0", "HS.q1", "HS.q2", "HS.q3", "HS.c0"]
    Bt4 = sb("Bt4", [128, 4, 2, 32])

    def s5_statein(qt):
        w = stream(s5scr[qt], SLOT)
        wv = w[:, :].rearrange("p (g s r m) -> p g s r m", g=2, s=L, r=2)
        pbs = [bank() for _ in range(4)]
        mms = []
        for g2 in range(2):
            gt = 2 * qt + g2
            uv = u5[gt][:, :].rearrange("p (c s) -> p c s", s=L)
            for ri in range(2):
                c0 = (ri * 2 + g2) * NCH
                for s in range(L):
                    for j in range(4):
                        mms.append((pbs[j][:, c0:c0 + NCH],
                                    wv[32 * j:32 * j + 32, g2, s, ri, :],
                                    uv[32 * j:32 * j + 32, :, s],
                                    s == 0, s == L - 1, (32 * j, 0)))
        b.mm(mms)
        for j in range(4):
            for ri in range(2):
                b.act(HSv[:, 1:NCH + 1, ri, 2 * qt:2 * qt + 2, j],
                      pbs[j][:, ri * 128:(ri + 1) * 128].rearrange("p (g c) -> p c g", g=2), AF.Copy,
                      wk=["HS.q%d" % qt])

    def s5_prefix_reduce_q(qt):
        cast_some(1)
        ps8 = slice(8 * qt, 8 * qt + 8)
        hk = ["HS.q%d" % qt]
        xv_ = lambda ri: HS[:, 1:NCH + 1, ri, ps8].rearrange("p (b k) j -> p b k j", b=4)
        pw_ = lambda ri: PW[:, :, ri, ps8].unsqueeze(1).broadcast_to([128, 4, 16, 8])
        v4 = lambda t: t[:, :].rearrange("p (b k j) -> p b k j", b=4, k=16)
        r4 = lambda t: t[:, :].rearrange("p (b k j) -> p b j k", b=4, k=16)
        t1, t2, t3, t4 = tmpf[0], tmpf[1], cbuf[0], cbuf[1]
        b.tt("dve", v4(t1), pw_(0), xv_(0), ALU.mult, rk=hk + ["PW"])
        b.tt("dve", v4(t2), pw_(1), xv_(1), ALU.mult, rk=hk + ["PW"])
        b.stt(t1[:, :], t2[:, :], -1.0, t1[:, :], ALU.mult, ALU.add)
        b.P.op("dve", lambda e, o=Bt4[:, :, 0, ps8], i=r4(t1): e.tensor_reduce(out=o, in_=i, axis=mybir.AxisListType.X, op=ALU.add),
               reads=[t1.name], writes=["Bt4.q%d" % qt])
        b.tt("pool", v4(t3), pw_(0), xv_(1), ALU.mult, rk=hk + ["PW"])
        b.tt("pool", v4(t4), pw_(1), xv_(0), ALU.mult, rk=hk + ["PW"])
        b.tt("pool", t3[:, :], t3[:, :], t4[:, :], ALU.add)
        b.P.op("dve", lambda e, o=Bt4[:, :, 1, ps8], i=r4(t3): e.tensor_reduce(out=o, in_=i, axis=mybir.AxisListType.X, op=ALU.add),
               reads=[t3.name], writes=["Bt4i.q%d" % qt])

    def s5_prefix_combine():
        bk = ["Bt4.q%d" % q for q in range(4)] + ["Bt4i.q%d" % q for q in range(4)]
        for blk in range(4):
            b.tt("dve", rt1[:, :, :], A1x[:, :, :], HS[:, 0, :, :], ALU.mult, rk=["A1x", "HS.c0"])
            b.tt("dve", rt2[:, 0, :], A2x[:, 0, :], HS[:, 0, 1, :], ALU.mult, rk=["A2x", "HS.c0"])
            b.tt("dve", rt2[:, 1, :], A2x[:, 1, :], HS[:, 0, 0, :], ALU.mult, rk=["A2x", "HS.c0"])
            b.tt("dve", rt1[:, :, :], rt1[:, :, :], rt2[:, :, :], ALU.add)
            b.tt("dve", HS[:, 0, :, :], rt1[:, :, :], Bt4[:, blk, :, :], ALU.add, rk=["rt1"] + bk, wk=["HS.c0"])

    def s5_recur(eng):
        for c in range(1, NCH + 1):
            b.tt(eng, rt1[:, :, :], A1[:, :, :], HS[:, c - 1, :, :], ALU.mult)
            b.tt(eng, rt2[:, 0, :], A2[:, 0, :], HS[:, c - 1, 1, :], ALU.mult)
            b.tt(eng, rt2[:, 1, :], A2[:, 1, :], HS[:, c - 1, 0, :], ALU.mult)
            b.tt(eng, HS[:, c, :, :], HS[:, c, :, :], rt1[:, :, :], ALU.add)
            b.tt(eng, HS[:, c, :, :], HS[:, c, :, :], rt2[:, :, :], ALU.add)

    def s5_carry(eng="pool"):
        b.copy(eng, HS[:, 0, :, :], HS[:, NCH, :, :])

    def s5_hb(eng):
        for ri in range(2):
            b.copy(eng, Hb[ri][:, :, :], HS[:, 0:NCH, ri, :].rearrange("p c j -> p j c"))
        s5_carry(eng)

    def s5_back():
        cc = C0[0] // L
        for gt in range(KT):
            w = stream(s5scr[4 + gt, :, 0:3072], 3072)
            kv = w[:, 0:1024].rearrange("p (k c) -> p k c", k=L)
            vv = w[:, 1024:3072].rearrange("p (j s r m) -> p j s r m", j=4, s=L, r=2)
            pb = bank()
            yv = pb[:, :].rearrange("p (c s) -> p c s", s=L)
            uv = u5[gt][:, :].rearrange("p (c s) -> p c s", s=L)
            mms = [(W(pb), kv[:, 0, :], W(u5[gt]), True, False, None)]
            for j in range(4):
                for s in range(L):
                    for ri in range(2):
                        mms.append((yv[32 * j:32 * j + 32, cc:NCH, s], vv[:, j, s, ri, :],
                                    Hb[ri][:, 4 * gt + j, cc:NCH], False, False, (0, 32 * j)))
            for k in range(1, L):
                if SIM_SAFE:
                    for s in range(k, L):
                        mms.append((yv[:, cc:NCH, s], kv[:, k, :], uv[:, cc:NCH, s - k], False, False, None))
                else:
                    mms.append((yv[:, cc:NCH, k:L], kv[:, k, :], uv[:, cc:NCH, 0:L - k], False, False, None))
            last = mms[-1]
            mms[-1] = (last[0], last[1], last[2], False, True, last[5])
            b.mm(mms)
            b.act(W(z[gt]), W(pb), AF.Gelu_apprx_tanh)

    def glu(hb, hook=None, add_eng="pool"):
        pend = []

        def stage_b(i, t):
            b.tt(add_eng, W(hb[i]), W(hb[i]), W(t), ALU.add)
            square(hb, i)

        for blk in range(4):
            if blk == 1 and hook is not None:
                hook()
            w = stream(wbf[B_GLU + blk], SLOT)
            wv = w[:, :].rearrange("p (i a k c) -> p i a k c", i=2, a=2, k=KT)
            for i2 in range(2):
                i = 2 * blk + i2
                pa, pg = bank(), bank()
                b.mm([(W(pa), wv[:, i2, 0, k, :], W(z[k]), k == 0, k == KT - 1, None) for k in range(KT)])
                b.mm([(W(pg), wv[:, i2, 1, k, :], W(z[k]), k == 0, k == KT - 1, None) for k in range(KT)])
                s = sbuf_[i % 2]
                b.act(W(s), W(pg), AF.Sigmoid)
                t = tmpf[i % 2]
                b.tt("dve", W(t), W(pa), W(s), ALU.mult)
                if pend:
                    stage_b(*pend.pop())
                pend.append((i, t))
        stage_b(*pend.pop())

    def ffn(hb, layer, par, filler=None, sqr=True):
        b_up = (B_UP0, B_UP1)[layer]
        b_dn = (B_DN0, B_DN1)[layer]
        halo_old, halo_new = haloF[layer][par], haloF[layer][1 - par]
        c0 = C0[0]
        pend = []

        def stage_b(f, pv):
            cb, sg = cbuf[f % 2], sbuf_[f % 2]
            bcol = C_FB + layer * FT + f
            b.act(W(sg), W(cb), AF.Silu, bias=cvec[:, bcol:bcol + 1])
            b.tt("dve", W(actb[f]), W(sg), W(pv), ALU.mult)

        for blk in range(FT // 2):
            w = stream(wbf[b_up + blk], SLOT)
            wv = w[:, :].rearrange("p (f a k c) -> p f a k c", f=2, a=2, k=KT)
            for f2 in range(2):
                f = 2 * blk + f2
                pg, pv = bank(), bank()
                b.mm([(W(pg), wv[:, f2, 0, k, :], W(u[k]), k == 0, k == KT - 1, None) for k in range(KT)])
                b.mm([(W(pv), wv[:, f2, 1, k, :], W(u[k]), k == 0, k == KT - 1, None) for k in range(KT)])
                gb, cb = gbuf[f % 2], cbuf[f % 2]
                wc = lambda kk: cvec[:, C_FW + (layer * 3 + kk) * FT + f: C_FW + (layer * 3 + kk) * FT + f + 1]
                b.act(W(gb, 2, 2), W(pg), AF.Copy)
                b.act(gb[:, c0:c0 + 2], halo_old[:, f, :], AF.Copy)
                b.act(halo_new[:, f, :], pg[:, T - 2:T], AF.Copy)
                b.act(W(cb), W(pg), AF.Copy, scale=wc(2))
                b.stt(W(cb), W(gb, 1, 1), wc(1), W(cb), ALU.mult, ALU.add)
                b.stt(W(cb), W(gb, 0, 0), wc(0), W(cb), ALU.mult, ALU.add)
                if pend:
                    stage_b(*pend.pop())
                pend.append((f, pv))
        stage_b(*pend.pop())
        if filler is not None:
            filler()
        FH = FT // 2
        for hf in range(2):
            for bi in range(4):
                w = stream(wbf[b_dn + hf * 4 + bi, :, 0:2 * FH * 128], 2 * FH * 128)
                wv = w[:, 0:2 * FH * 128].rearrange("p (m f c) -> p m f c", m=2, f=FH)
                for m2 in range(2):
                    m = 2 * bi + m2
                    pb = bank()
                    b.mm([(W(pb), wv[:, m2, f, :], W(actb[hf * FH + f]), f == 0, f == FH - 1, None)
                          for f in range(FH)])
                    b.tt("dve", W(hb[m]), W(hb[m]), W(pb), ALU.add)
                    if sqr and hf == 1:
                        square(hb, m)

    def shortconv(hb, par):
        halo_old, halo_new = haloS[par], haloS[1 - par]
        c0 = C0[0]
        for i in range(KT):
            w = stream(wbf[B_IN + i, :, 0:3072], 3072)
            wv = w[:, 0:3072].rearrange("p (a k c) -> p a k c", a=3, k=KT)
            pbg, pcg, phh = bank(), bank(), bank()
            for a, pp in ((0, pbg), (1, pcg), (2, phh)):
                b.mm([(W(pp), wv[:, a, k, :], W(u[k]), k == 0, k == KT - 1, None) for k in range(KT)])
            cg, pbf, cb = cgs[i % 2], pbuf[i % 2], cbuf[i % 2]
            b.act(W(cg), W(pcg), AF.Copy)
            b.tt("dve", W(pbf, 2, 2), W(cg), W(phh), ALU.mult)
            b.act(pbf[:, c0:c0 + 2], halo_old[:, i, :], AF.Copy)
            b.act(halo_new[:, i, :], pbf[:, T:T + 2], AF.Copy)
            wc = lambda kk: cvec[:, C_SCW + kk * KT + i: C_SCW + kk * KT + i + 1]
            b.ts("dve", W(cb), W(pbf, 2, 2), wc(2), ALU.mult)
            b.stt(W(cb), W(pbf, 1, 1), wc(1), W(cb), ALU.mult, ALU.add)
            b.stt(W(cb), W(pbf, 0, 0), wc(0), W(cb), ALU.mult, ALU.add)
            b.tt("dve", W(q[i]), W(cb), W(pbg), ALU.mult)
        for blk in range(2):
            w = stream(wbf[B_OUT + blk], SLOT)
            wv = w[:, :].rearrange("p (m k c) -> p m k c", m=4, k=KT)
            for m4 in range(4):
                m = 4 * blk + m4
                pb = bank()
                b.mm([(W(pb), wv[:, m4, k, :], W(q[k]), k == 0, k == KT - 1, None) for k in range(KT)])
                b.tt("dve", W(hb[m]), W(hb[m]), W(pb), ALU.add)
                square(hb, m)

    xv = xT.rearrange("(k p) t -> k p t", p=128)
    ov = outT.rearrange("(k p) t -> k p t", p=128)
    def load_x(ti, eng="sp"):
        hb = h[ti % 2]
        return [b.dma(eng, hb[k][:, :], xv[k, :, ti * T:(ti + 1) * T], hb[k].name) for k in range(KT)]

    def front_a(ti):
        rmsnorm(h[ti % 2], C_NM0, dst=u5, sqb=sq5)

    def front_sq(ti):
        for k in range(KT):
            b.act(sq5[k][:, :], h[ti % 2][k][:, :], AF.Square)

    def front_rest(ti):
        rmsnorm(h[ti % 2], C_NM0, dst=u5, sqb=sq5, presq=True)

    for ti in range(npre + 1):
        evs = load_x(ti)
        if ti == npre:
            cast_some(NBLK)
            for g in range(7):
                P.seal("cast%d" % g)
        front_a(ti)
        for qt in range(4):
            s5_statein(qt)
            if ti < npre:
                s5_prefix_reduce_q(qt)
        if ti < npre:
            s5_prefix_combine()
        else:
            s5_recur("dve")
            s5_hb("dve")
    for ti in range(npre, ntile):
        hb = h[ti % 2]
        par = (ti - npre) % 2
        nxt = ti + 1 < ntile
        C0[0] = (T - WARM_COLS) if ti == npre else 0
        if ti == npre and nxt:
            s5_back()
            load_x(ti + 1)
            front_a(ti + 1)
            for qt in range(4):
                s5_statein(qt)
            s5_recur("pool")
            glu(hb, add_eng="dve")
            rmsnorm(hb, C_NF0, presq=True)
            ffn(hb, 0, par)
            rmsnorm(hb, C_NM1, presq=True)
        else:
            if nxt:
                load_x(ti + 1, "pool")
            s5_back()
            if nxt:
                front_sq(ti + 1)
            glu(hb, hook=(lambda: front_rest(ti + 1)) if nxt else None)
            rmsnorm(hb, C_NF0, presq=True, filler=(lambda: (s5_statein(0), s5_statein(1))) if nxt else None)
            ffn(hb, 0, par)
            rmsnorm(hb, C_NM1, presq=True, filler=(lambda: (s5_statein(2), s5_statein(3))) if nxt else None)
            if nxt:
                s5_recur("pool")
        shortconv(hb, par)
        rmsnorm(hb, C_NF1, presq=True)
        ffn(hb, 1, par, filler=(lambda: s5_hb("act")) if nxt else None)
        if ti > npre:
            mi = ti - npre - 1
            rmsnorm(hb, C_NFIN, final=True, presq=True)
            for k in range(KT):
                b.dma("pool", ov[k, :, mi * T:(mi + 1) * T], hb[k][:, :], "o" + hb[k].name)
    P.emit()
    return nc


def _cols(v):
    return np.ascontiguousarray(v.reshape(-1, 128).T)


def prepare_shared(inp):
    f32 = np.float32
    wst = np.zeros((NBLK, 128, SLOT), f32)
    wg = inp["s5_w_glu"][0]
    for blk in range(4):
        v = wst[B_GLU + blk].reshape(128, 2, 2, KT, 128)
        for i2 in range(2):
            i = 2 * blk + i2
            for a in range(2):
                c0 = a * D + i * 128
                v[:, i2, a] = wg[:, c0:c0 + 128].reshape(KT, 128, 128).transpose(1, 0, 2)
    for layer in range(2):
        wu = inp["ffn_w_up"][layer]
        wd = inp["ffn_w_down"][layer]
        b_up = (B_UP0, B_UP1)[layer]
        b_dn = (B_DN0, B_DN1)[layer]
        for blk in range(FT // 2):
            v = wst[b_up + blk].reshape(128, 2, 2, KT, 128)
            for f2 in range(2):
                f = 2 * blk + f2
                for a in range(2):
                    c0 = a * FF + f * 128
                    v[:, f2, a] = wu[:, c0:c0 + 128].reshape(KT, 128, 128).transpose(1, 0, 2)
        FH = FT // 2
        for hf in range(2):
            for bi in range(4):
                v = wst[b_dn + hf * 4 + bi][:, 0:2 * FH * 128].reshape(128, 2, FH, 128)
                for m2 in range(2):
                    m = 2 * bi + m2
                    v[:, m2] = wd[hf * FH * 128:(hf + 1) * FH * 128, m * 128:(m + 1) * 128].reshape(FH, 128, 128).transpose(1, 0, 2)
    wi = inp["sc_w_in"][0]
    wo = inp["sc_w_out"][0]
    for i in range(KT):
        v = wst[B_IN + i][:, 0:3072].reshape(128, 3, KT, 128)
        for a in range(3):
            c0 = a * D + i * 128
            v[:, a] = wi[:, c0:c0 + 128].reshape(KT, 128, 128).transpose(1, 0, 2)
    for blk in range(2):
        v = wst[B_OUT + blk].reshape(128, 4, KT, 128)
        for m4 in range(4):
            m = 4 * blk + m4
            v[:, m4] = wo[:, m * 128:(m + 1) * 128].reshape(KT, 128, 128).transpose(1, 0, 2)

    cvec = np.zeros((128, NCV), f32)
    cvec[:, C_NM0:C_NM0 + 8] = _cols(inp["norm_mix"][0])
    cvec[:, C_NM1:C_NM1 + 8] = _cols(inp["norm_mix"][1])
    cvec[:, C_NF0:C_NF0 + 8] = _cols(inp["norm_ffn"][0])
    cvec[:, C_NF1:C_NF1 + 8] = _cols(inp["norm_ffn"][1])
    cvec[:, C_NFIN:C_NFIN + 8] = _cols(inp["norm_final"])
    cvec[:, C_D:C_D + 8] = _cols(inp["s5_d"][0])
    for kk in range(3):
        cvec[:, C_SCW + kk * KT:C_SCW + (kk + 1) * KT] = _cols(inp["sc_conv_w"][0, kk])
    for layer in range(2):
        for kk in range(3):
            c0 = C_FW + (layer * 3 + kk) * FT
            cvec[:, c0:c0 + FT] = _cols(inp["ffn_conv_w"][layer, kk])
        c0 = C_FB + layer * FT
        cvec[:, c0:c0 + FT] = _cols(inp["ffn_conv_b"][layer])

    cst = np.zeros((128, 264), f32)
    pidx = np.arange(128)
    cst[:, 0:128] = (pidx[:, None] // 16 == pidx[None, :] // 16)
    cst[:, 128:256] = np.eye(128)
    cst[:, 256] = ((pidx // 16) % 2 == 0)
    cst[:, 257] = ((pidx // 16) % 2 == 1)
    cst[:, 258] = (pidx < 64)
    cst[:, 259] = (pidx >= 64)
    cst[:, 260] = -1.0 * (pidx < 64)
    cst[:, 261] = -1.0 * (pidx >= 64)

    a_re, a_im, ldt = inp["s5_a_re"][0], inp["s5_a_im"][0], inp["s5_log_dt"][0]
    b_re, b_im = inp["s5_b_re"][0], inp["s5_b_im"][0]
    c_re, c_im = inp["s5_c_re"][0], inp["s5_c_im"][0]

    def layA_b(x):
        return np.ascontiguousarray(x.reshape(8, 8, 64, 16).transpose(1, 3, 0, 2).reshape(128, 512))

    def layA_a(x):
        y = x.reshape(8, 8, 64).transpose(1, 0, 2)
        return np.ascontiguousarray(np.broadcast_to(y[:, None], (8, 16, 8, 64)).reshape(128, 512))

    ldt_gp = np.broadcast_to(ldt[:, None], (64, 64))
    s5A = np.stack([layA_b(b_re), layA_b(b_im), layA_a(a_re), layA_a(a_im), layA_a(ldt_gp)]).astype(f32)

    def layP_a(x):
        return np.ascontiguousarray(x.reshape(32, 2, 64).transpose(1, 2, 0).reshape(128, 32))

    def layP_c(x):
        return np.ascontiguousarray(x.reshape(32, 2, 16, 64).transpose(1, 3, 0, 2).reshape(128, 512))

    s5Ps = np.stack([layP_a(a_re), layP_a(a_im), layP_a(ldt_gp)]).astype(f32)
    s5Pc = np.stack([layP_c(c_re), layP_c(c_im)]).astype(f32)
    s5Qs = np.stack([a_re.T, a_im.T, ldt_gp.T]).astype(f32)
    s5Qb = np.stack([b_re.transpose(1, 0, 2).reshape(64, 1024), b_im.transpose(1, 0, 2).reshape(64, 1024),
                     c_re.transpose(2, 0, 1).reshape(64, 1024), c_im.transpose(2, 0, 1).reshape(64, 1024)]).astype(f32)
    return {"wst": wst, "cvec": cvec, "cst": cst, "s5A": np.ascontiguousarray(s5A), "s5Ps": np.ascontiguousarray(s5Ps),
            "s5Pc": np.ascontiguousarray(s5Pc), "s5Qs": np.ascontiguousarray(s5Qs), "s5Qb": np.ascontiguousarray(s5Qb)}


NPRE = 7
NMAIN = 8
SIM_SAFE = False
WARM_COLS = 16
STOP = None


def kernel(**inputs):
    inp = {k: np.asarray(v) for k, v in inputs.items()}
    x = inp["x"].astype(np.float32, copy=False)
    bsz, seq, d = x.shape
    half = seq // 2
    shared = prepare_shared(inp)
    in_maps = []
    for c in range(8):
        bi, hf = c // 2, c % 2
        xt = np.zeros((D, 2 * half), np.float32)
        if hf == 1:
            xt[:, 0:half] = x[bi, 0:half].T
        xt[:, half:] = x[bi, hf * half:(hf + 1) * half].T
        m = dict(shared)
        m["xT"] = xt
        in_maps.append(m)
    nc = build_program(NPRE, NMAIN)
    res = run_bass_kernel_spmd(nc, in_maps, core_ids=list(range(8)))
    out = np.empty((bsz, seq, d), np.float32)
    for c in range(8):
        bi, hf = c // 2, c % 2
        out[bi, hf * half:(hf + 1) * half] = res.results[c]["outT"].T
    return out
```

```python
import contextlib
import numpy as np
import concourse.bass as bass
import concourse.mybir as mybir
from concourse.bass_utils import run_bass_kernel_spmd

F32 = mybir.dt.float32
BF16 = mybir.dt.bfloat16
AF = mybir.ActivationFunctionType
ALU = mybir.AluOpType

D = 1024
KT = 8
FF = 2816
FT = 22
T = 512
L = 8
NCH = T // L
NS = 5
SLOT = 4096
RMS_EPS = 1e-6
ENGS = ("pe", "act", "dve", "pool", "sp")

B_GLU, B_UP0, B_DN0, B_IN, B_OUT, B_UP1, B_DN1, NBLK = 0, 4, 15, 23, 31, 33, 44, 52
C_NM0, C_NM1, C_NF0, C_NF1, C_NFIN, C_D, C_SCW, C_FW, C_FB, NCV = 0, 8, 16, 24, 32, 40, 48, 72, 204, 248


class Prog:
    def __init__(self, nc):
        self.nc = nc
        self.cnt = {e: 0 for e in ENGS}
        self.ops = {e: [] for e in ENGS}
        self.writer = {}
        self.readers = {}
        self.seen = {e: {} for e in ENGS}
        self.pending = {e: {} for e in ENGS}
        self.chan_cnt = {}
        self.chan_events = {}
        self.sems = {}
        self.alias = {}

    def _need(self, eng, key, val, waits):
        if key == eng and eng == "pe":
            return
        if self.seen[eng].get(key, 0) >= val:
            return
        waits[key] = max(waits.get(key, 0), val)

    def op(self, eng, fn, reads=(), writes=(), chan=None, after=()):
        reads = [k for n in reads for k in self.alias.get(n, (n,))]
        writes = [k for n in writes for k in self.alias.get(n, (n,))]
        waits = {}
        for ev in after:
            self._need(eng, ev[0], ev[1], waits)
        for k, v in self.pending[eng].items():
            self._need(eng, k, v, waits)
        self.pending[eng] = {}
        for b in reads:
            ev = self.writer.get(b)
            if ev is not None:
                self._need(eng, ev[0], ev[1], waits)
        for b in writes:
            ev = self.writer.get(b)
            if ev is not None:
                self._need(eng, ev[0], ev[1], waits)
            for ev in self.readers.get(b, ()):
                self._need(eng, ev[0], ev[1], waits)
        for k, v in waits.items():
            self.seen[eng][k] = v
        if chan is None:
            self.cnt[eng] += 1
            ev = [eng, self.cnt[eng]]
        else:
            key = "c:" + chan
            self.chan_cnt[key] = self.chan_cnt.get(key, 0) + 16
            ev = [key, self.chan_cnt[key]]
            self.chan_events.setdefault(key, []).append(ev)
        for b in writes:
            self.writer[b] = ev
            self.readers[b] = []
        for b in reads:
            if b not in writes:
                self.readers.setdefault(b, []).append(ev)
        self.ops[eng].append((waits, fn, ev))
        return ev

    def seal(self, chan):
        key = "c:" + chan
        tot = self.chan_cnt.get(key, 0)
        for ev in self.chan_events.get(key, []):
            ev[1] = tot

    def barrier(self):
        snap = dict(self.cnt)
        snap.update({k: v for k, v in self.chan_cnt.items() if not k.startswith("c:cast")})
        for e in ENGS:
            p = self.pending[e]
            for k, v in snap.items():
                if v > 0 and k != e:
                    p[k] = max(p.get(k, 0), v)

    def emit(self):
        nc = self.nc
        keys = sorted(set(ENGS) | set(self.chan_cnt.keys()))
        with contextlib.ExitStack() as st:
            for i, k in enumerate(keys):
                self.sems[k] = st.enter_context(nc.semaphore("s%d" % i))
            fin = {e: self.cnt[e] for e in ENGS if e != "sp" and self.cnt[e] > 0}
            fin.update(self.chan_cnt)
            with nc.Block() as block:
                def run(eng_name, eng):
                    for waits, fn, ev in self.ops[eng_name]:
                        for k, v in waits.items():
                            eng.wait_ge(self.sems[k], v)
                        ins = fn(eng)
                        ins.then_inc(self.sems[ev[0]], 16 if ev[0].startswith("c:") else 1)
                    if eng_name == "sp":
                        for k, v in fin.items():
                            eng.wait_ge(self.sems[k], v)

                @block.tensor
                def _(eng):
                    run("pe", eng)

                @block.scalar
                def _(eng):
                    run("act", eng)

                @block.vector
                def _(eng):
                    run("dve", eng)

                @block.gpsimd
                def _(eng):
                    run("pool", eng)

                @block.sync
                def _(eng):
                    run("sp", eng)


def _nm(*aps):
    return [a.name for a in aps if hasattr(a, "name") and not isinstance(a, (int, float))]


class B:
    def __init__(self, P):
        self.P = P

    def tt(self, eng, out, in0, in1, op, rk=None, wk=None):
        self.P.op(eng, lambda e: e.tensor_tensor(out=out, in0=in0, in1=in1, op=op),
                  reads=_nm(in0, in1) if rk is None else rk, writes=_nm(out) if wk is None else wk)

    def ts(self, eng, out, in0, s1, op0, s2=None, op1=None):
        if op1 is None:
            s2, op1 = 0.0, ALU.add
        fn = lambda e: e.tensor_scalar(out=out, in0=in0, scalar1=s1, scalar2=s2, op0=op0, op1=op1)
        self.P.op(eng, fn, reads=_nm(in0, s1, s2), writes=_nm(out))

    def stt(self, out, in0, scalar, in1, op0, op1):
        self.P.op("dve", lambda e: e.scalar_tensor_tensor(out=out, in0=in0, scalar=scalar, in1=in1, op0=op0, op1=op1),
                  reads=_nm(in0, scalar, in1), writes=_nm(out))

    def act(self, out, in_, func, bias=None, scale=None, wk=None):
        kw = {}
        if bias is not None:
            kw["bias"] = bias
        if scale is not None:
            kw["scale"] = scale
        self.P.op("act", lambda e: e.activation(out=out, in_=in_, func=func, **kw),
                  reads=_nm(in_, bias, scale), writes=_nm(out) if wk is None else wk)

    def copy(self, eng, out, in_):
        if eng == "act":
            return self.act(out, in_, AF.Copy)
        self.P.op(eng, lambda e: e.tensor_copy(out=out, in_=in_), reads=_nm(in_), writes=_nm(out))

    def recip(self, out, in_):
        self.P.op("dve", lambda e: e.reciprocal(out=out, in_=in_), reads=_nm(in_), writes=_nm(out))

    def memset(self, eng, ap, val):
        self.P.op(eng, lambda e: e.memset(ap, val), writes=_nm(ap))

    def dma(self, eng, out, in_, chan, after=()):
        return self.P.op(eng, lambda e: e.dma_start(out=out, in_=in_), reads=_nm(in_), writes=_nm(out), chan=chan,
                         after=after)

    def mm(self, mms):
        reads, writes = [], []
        for m in mms:
            writes += _nm(m[0])
            reads += _nm(m[1], m[2])

        def fn(e):
            ins = None
            for (out, lhsT, rhs, start, stop, tp) in mms:
                if tp is None:
                    ins = e.matmul(out, lhsT, rhs, start=start, stop=stop)
                else:
                    ins = e.matmul(out, lhsT, rhs, start=start, stop=stop, tile_position=tp)
            return ins
        self.P.op("pe", fn, reads=sorted(set(reads)), writes=sorted(set(writes)))


def build_program(npre, nmain):
    ntile = npre + 1 + nmain
    ntok = ntile * T
    nc = bass.Bass("TRN2", target_bir_lowering=False)
    dt_in = lambda name, shape: nc.dram_tensor(name, shape, F32, kind="ExternalInput").ap()
    xT = dt_in("xT", [D, ntok])
    wst = dt_in("wst", [NBLK, 128, SLOT])
    cvec_d = dt_in("cvec", [128, NCV])
    cst_d = dt_in("cst", [128, 264])
    s5A_d = dt_in("s5A", [5, 128, 512])
    s5Ps_d = dt_in("s5Ps", [3, 128, 32])
    s5Pc_d = dt_in("s5Pc", [2, 128, 512])
    s5Qs_d = dt_in("s5Qs", [3, 64, 64])
    s5Qb_d = dt_in("s5Qb", [4, 64, 1024])
    outT = nc.dram_tensor("outT", [D, nmain * T], F32, kind="ExternalOutput").ap()
    GROUPS = [B_GLU, B_UP0, B_DN0, B_IN, B_OUT, B_UP1, B_DN1, NBLK]
    wbf_g = [nc.dram_tensor("wbf%d" % g, [GROUPS[g + 1] - GROUPS[g], 128, SLOT], BF16, kind="Internal").ap()
             for g in range(7)]

    class _WBF:
        def __getitem__(self, key):
            j = key[0] if isinstance(key, tuple) else key
            g = max(i for i in range(7) if GROUPS[i] <= j)
            blk = wbf_g[g][j - GROUPS[g]]
            return blk[key[1:]] if isinstance(key, tuple) else blk
    wbf = _WBF()
    grp_of = lambda j: max(i for i in range(7) if GROUPS[i] <= j)
    s5scr = nc.dram_tensor("s5scr", [12, 128, SLOT], BF16, kind="Internal").ap()

    P = Prog(nc)
    b = B(P)
    sb = lambda name, shape, dt=F32: nc.alloc_sbuf_tensor(name, shape, dt)

    cvec = sb("cvec_s", [128, NCV])
    cst = sb("cst_s", [128, 264])
    ones_bf = sb("ones_bf", [128, 128], BF16)
    eps_t = sb("eps_t", [128, 1])
    hpi_t = sb("hpi_t", [128, 1])
    A1 = sb("A1", [128, 2, 32])
    A2 = sb("A2", [128, 2, 32])
    PW = sb("PW", [128, 16, 2, 32])
    A1x = sb("A1x", [128, 2, 32])
    A2x = sb("A2x", [128, 2, 32])
    haloF = [[sb("haloF%d_%d" % (l, i), [128, FT, 2]) for i in range(2)] for l in range(2)]
    haloS = [sb("haloS%d" % i, [128, KT, 2]) for i in range(2)]
    ps = [nc.alloc_psum_tensor("ps%d" % i, [128, 512], F32) for i in range(8)]

    blockmask = cst[:, 0:128]
    ident = cst[:, 128:256]
    maskE = cst[:, 256:258]
    maskH = cst[:, 258:262]

    b.dma("sp", cvec[:, :], cvec_d, "const")
    b.dma("sp", cst[:, :], cst_d, "const")
    b.memset("pool", ones_bf[:, :], 1.0)
    b.memset("pool", eps_t[:, :], RMS_EPS)
    b.memset("pool", hpi_t[:, :], float(np.pi / 2))
    for l in range(2):
        for i in range(2):
            b.memset("pool", haloF[l][i][:, :, :], 0.0)
    for i in range(2):
        b.memset("pool", haloS[i][:, :, :], 0.0)
    cast_state = {"next": 0}

    def cast_some(n, after=()):
        for _ in range(n):
            j = cast_state["next"]
            if j >= NBLK:
                return
            b.dma("pool", wbf[j], wst[j], "cast%d" % grp_of(j), after=after)
            cast_state["next"] = j + 1

    with contextlib.ExitStack() as tmp:
        def tsb(name, shape, dt=F32):
            return tmp.enter_context(nc.sbuf_tensor(name, shape, dt))

        def cmul(o_r, o_i, a_r, a_i, b_r, b_i, t1, t2, eng="dve", castn=0):
            cast_some(castn)
            b.tt(eng, t1, a_r, b_r, ALU.mult)
            b.tt(eng, t2, a_i, b_i, ALU.mult)
            b.tt(eng, o_r, t1, t2, ALU.subtract)
            b.tt(eng, t1, a_r, b_i, ALU.mult)
            b.tt(eng, t2, a_i, b_r, ALU.mult)
            b.tt(eng, o_i, t1, t2, ALU.add)

        def make_a(pfx, lr, li, ldt, pp, n, want_g, eng="dve"):
            mk = lambda s: tsb(pfx + s, [128, n])[0:pp, :]
            dtt, xr, xi, zr, zi, t1, t2, t3 = [mk(s) for s in ("dt", "xr", "xi", "zr", "zi", "t1", "t2", "t3")]
            b.act(dtt, ldt, AF.Exp)
            b.tt(eng, xr, lr, dtt, ALU.mult)
            b.tt(eng, xi, li, dtt, ALU.mult)
            b.act(t1, xr, AF.Exp, scale=1.0 / 16)
            b.act(t2, xi, AF.Sin, scale=1.0 / 16)
            b.act(t3, xi, AF.Sin, scale=1.0 / 16, bias=hpi_t[0:pp, :])
            b.tt(eng, zr, t1, t3, ALU.mult)
            b.tt(eng, zi, t1, t2, ALU.mult)
            for _ in range(4):
                b.tt(eng, t1, zr, zr, ALU.mult)
                b.tt(eng, t2, zi, zi, ALU.mult)
                b.tt(eng, t3, zr, zi, ALU.mult)
                b.tt(eng, zr, t1, t2, ALU.subtract)
                b.tt(eng, zi, t3, t3, ALU.add)
            if not want_g:
                return zr, zi, None, None
            gr, gi, nr, rden = [mk(s) for s in ("gr", "gi", "nr", "rden")]
            b.tt(eng, t1, lr, lr, ALU.mult)
            b.tt(eng, t2, li, li, ALU.mult)
            b.tt(eng, t1, t1, t2, ALU.add)
            b.recip(rden, t1)
            b.ts(eng, nr, zr, -1.0, ALU.add)
            b.tt(eng, t1, nr, lr, ALU.mult)
            b.tt(eng, t2, zi, li, ALU.mult)
            b.tt(eng, t1, t1, t2, ALU.add)
            b.tt(eng, gr, t1, rden, ALU.mult)
            b.tt(eng, t1, zi, lr, ALU.mult)
            b.tt(eng, t2, nr, li, ALU.mult)
            b.tt(eng, t1, t1, t2, ALU.subtract)
            b.tt(eng, gi, t1, rden, ALU.mult)
            return zr, zi, gr, gi

        inA = [tsb("inA%d" % i, [128, 512]) for i in range(5)]
        for i in range(5):
            b.dma("sp", inA[i][:, :], s5A_d[i], "const")
        inPs = [tsb("inPs%d" % i, [128, 32]) for i in range(3)]
        for i in range(3):
            b.dma("sp", inPs[i][:, :], s5Ps_d[i], "const")
        inPc = [tsb("inPc%d" % i, [128, 512]) for i in range(2)]
        for i in range(2):
            b.dma("sp", inPc[i][:, :], s5Pc_d[i], "const")
        inQs = [tsb("inQs%d" % i, [128, 64]) for i in range(3)]
        for i in range(3):
            b.dma("sp", inQs[i][0:64, :], s5Qs_d[i], "const")
        inQb = [tsb("inQb%d" % i, [128, 1024]) for i in range(4)]
        for i in range(4):
            b.dma("sp", inQb[i][0:64, :], s5Qb_d[i], "const")
        P.seal("const")

        aAr, aAi, gAr, gAi = make_a("A_", inA[2][:, :], inA[3][:, :], inA[4][:, :], 128, 512, True)
        aPr, aPi, _, _ = make_a("P_", inPs[0][:, :], inPs[1][:, :], inPs[2][:, :], 128, 32, False)
        aQr, aQi, gQr, gQi = make_a("Q_", inQs[0][0:64, :], inQs[1][0:64, :], inQs[2][0:64, :], 64, 64, True)
        cast_some(14)
        Wr = [tsb("WrA%d" % i, [128, 512]) for i in range(2)]
        Wi = [tsb("WiA%d" % i, [128, 512]) for i in range(2)]
        tA1 = tsb("tA1", [128, 512])
        tA2 = tsb("tA2", [128, 512])
        BDt = [tsb("BDt%d" % i, [128, SLOT], BF16) for i in range(4)]
        cmul(Wr[0][:, :], Wi[0][:, :], gAr, gAi, inA[0][:, :], inA[1][:, :], tA1[:, :], tA2[:, :])
        cur = 0
        for s in range(L - 1, -1, -1):
            for blk in range(4):
                bdv = BDt[blk][:, :].rearrange("p (g s r e q) -> p g s r e q", g=2, s=L, r=2, e=2)
                for ri in range(2):
                    src = (Wr, Wi)[ri][cur][:, blk * 128:(blk + 1) * 128].rearrange("p (g q) -> p g q", g=2)
                    for e2 in range(2):
                        b.act(bdv[:, :, s, ri, e2, :], src, AF.Copy, scale=maskE[:, e2:e2 + 1])
            if s > 0:
                cmul(Wr[1 - cur][:, :], Wi[1 - cur][:, :], aAr, aAi, Wr[cur][:, :], Wi[cur][:, :], tA1[:, :], tA2[:, :])
                cur = 1 - cur
        for blk in range(4):
            b.dma("sp", s5scr[blk], BDt[blk][:, :], "scr")

        akr = [tsb("akr%d" % i, [128, 32]) for i in range(2)]
        aki = [tsb("aki%d" % i, [128, 32]) for i in range(2)]
        tP1 = tsb("tP1", [128, 32])
        tP2 = tsb("tP2", [128, 32])
        Vre = tsb("Vre", [128, 32, 16])
        Vim = tsb("Vim", [128, 32, 16])
        tV1 = tsb("tV1", [128, 32, 16])
        tV2 = tsb("tV2", [128, 32, 16])
        OUTt = tsb("OUTt", [128, 8, 3072], BF16)
        Cr = inPc[0][:, :].rearrange("p (j h) -> p j h", h=16)
        Ci = inPc[1][:, :].rearrange("p (j h) -> p j h", h=16)
        b.copy("dve", akr[0][:, :], aPr)
        b.copy("dve", aki[0][:, :], aPi)
        cur = 0
        ovv = OUTt[:, :, 1024:3072].rearrange("p g (j s r e h) -> p g j s r e h", j=4, s=L, r=2, e=2)
        for s in range(L):
            bcr = akr[cur][:, :].unsqueeze(2).broadcast_to([128, 32, 16])
            bci = aki[cur][:, :].unsqueeze(2).broadcast_to([128, 32, 16])
            b.tt("dve", tV1[:, :, :], Cr, bcr, ALU.mult)
            b.tt("dve", tV2[:, :, :], Ci, bci, ALU.mult)
            b.tt("dve", Vre[:, :, :], tV1[:, :, :], tV2[:, :, :], ALU.subtract)
            b.tt("dve", tV1[:, :, :], Cr, bci, ALU.mult)
            b.tt("dve", tV2[:, :, :], Ci, bcr, ALU.mult)
            b.tt("dve", Vim[:, :, :], tV1[:, :, :], tV2[:, :, :], ALU.add)
            for ri in range(2):
                src = (Vre, Vim)[ri][:, :, :].rearrange("p (g j) h -> p g j h", g=8)
                for e2 in range(2):
                    msk = maskH[:, 2 * ri + e2:2 * ri + e2 + 1]
                    b.act(ovv[:, :, :, s, ri, e2, :], src, AF.Copy, scale=msk)
            if s == L - 1:
                pass
            cmul(akr[1 - cur][:, :], aki[1 - cur][:, :], aPr, aPi, akr[cur][:, :], aki[cur][:, :], tP1[:, :], tP2[:, :])
            cur = 1 - cur
        aLr, aLi = akr[1 - cur][:, :], aki[1 - cur][:, :]
        b.copy("dve", A1[:, 0, :], aLr)
        b.copy("dve", A1[:, 1, :], aLr)
        b.copy("dve", A2[:, 1, :], aLi)
        b.ts("dve", A2[:, 0, :], aLi, -1.0, ALU.mult)
        b.memset("dve", PW[:, 15, 0, :], 1.0)
        b.memset("dve", PW[:, 15, 1, :], 0.0)
        for k in range(14, -1, -1):
            cmul(PW[:, k, 0, :], PW[:, k, 1, :], aLr, aLi, PW[:, k + 1, 0, :], PW[:, k + 1, 1, :], tP1[:, :], tP2[:, :])
        cmul(A1x[:, 0, :], A2x[:, 1, :], aLr, aLi, PW[:, 0, 0, :], PW[:, 0, 1, :], tP1[:, :], tP2[:, :])
        b.copy("dve", A1x[:, 1, :], A1x[:, 0, :])
        b.ts("dve", A2x[:, 0, :], A2x[:, 1, :], -1.0, ALU.mult)

        aBr = [tsb("aBr%d" % i, [128, 1024]) for i in range(2)]
        aBi = [tsb("aBi%d" % i, [128, 1024]) for i in range(2)]
        tQ1 = tsb("tQ1", [128, 1024])
        tQ2 = tsb("tQ2", [128, 1024])
        nCi = tsb("nCi", [128, 1024])
        q3 = lambda t: t[0:64, :].rearrange("p (g h) -> p g h", h=16)
        bq = lambda a: a.unsqueeze(2).broadcast_to([64, 64, 16])
        cmul(q3(aBr[0]), q3(aBi[0]), bq(gQr), bq(gQi), q3(inQb[0]), q3(inQb[1]), q3(tQ1), q3(tQ2), castn=2)
        b.ts("dve", nCi[0:64, :], inQb[3][0:64, :], -1.0, ALU.mult)
        cur = 0
        for k in range(L):
            for gt in range(KT):
                cs = slice(gt * 128, (gt + 1) * 128)
                reg = ps[gt][:, (k % 4) * 128:(k % 4 + 1) * 128]
                b.mm([(reg, aBr[cur][0:64, cs], inQb[2][0:64, cs], True, False, None),
                      (reg, aBi[cur][0:64, cs], nCi[0:64, cs], False, True, None)])
            if k % 4 == 3:
                for gt in range(KT):
                    o = OUTt[:, gt, (k - 3) * 128:(k + 1) * 128].rearrange("p (k c) -> p k c", k=4)
                    i0 = ps[gt][:, :].rearrange("p (k c) -> p k c", k=4)
                    b.tt("dve", o, i0, blockmask.unsqueeze(1).broadcast_to([128, 4, 128]), ALU.mult)
            if k < L - 1:
                cmul(q3(aBr[1 - cur]), q3(aBi[1 - cur]), bq(aQr), bq(aQi), q3(aBr[cur]), q3(aBi[cur]), q3(tQ1), q3(tQ2), castn=2)
                cur = 1 - cur
        for gt in range(KT):
            b.stt(OUTt[:, gt, 0:128], ident, cvec[:, C_D + gt:C_D + gt + 1], OUTt[:, gt, 0:128], ALU.mult, ALU.add)
            b.dma("sp", s5scr[4 + gt, :, 0:3072], OUTt[:, gt, :], "scr")
        P.seal("scr")
        P.barrier()
    h = [[sb("h%d_%d" % (i, k), [128, T]) for k in range(KT)] for i in range(2)]
    u = [sb("u%d" % k, [128, T], BF16) for k in range(KT)]
    sq = [sb("sq%d" % k, [128, T], BF16) for k in range(KT)]
    z = [sb("z%d" % k, [128, T], BF16) for k in range(KT)]
    q = z
    u5 = [sb("u5_%d" % k, [128, T], BF16) for k in range(KT)]
    sq5 = [sb("sq5_%d" % k, [128, T], BF16) for k in range(KT)]
    Bt = sb("Bt", [128, 2, 32])
    actb = [sb("act%d" % f, [128, T], BF16) for f in range(FT)]
    rstd = sb("rstd", [128, T])
    srt = sb("srt", [128, T])
    rstd5 = sb("rstd5", [128, T])
    srt5 = sb("srt5", [128, T])
    gbuf = [sb("gbuf%d" % i, [128, T + 2]) for i in range(2)]
    cbuf = [sb("cbuf%d" % i, [128, T]) for i in range(2)]
    sbuf_ = [sb("sbuf%d" % i, [128, T]) for i in range(2)]
    pbuf = [sb("pbuf%d" % i, [128, T + 2]) for i in range(2)]
    cgs = [sb("cgs%d" % i, [128, T]) for i in range(2)]
    tmpf = [sb("tmpf%d" % i, [128, T]) for i in range(2)]
    HS = sb("HS", [128, NCH + 1, 2, 32])
    Hb = [sb("Hb%d" % i, [128, 32, NCH], BF16) for i in range(2)]
    rt1 = sb("rt1", [128, 2, 32])
    rt2 = sb("rt2", [128, 2, 32])
    slots = [sb("slot%d" % i, [128, SLOT], BF16) for i in range(NS)]
    b.memset("pool", HS[:, 0, :, :], 0.0)

    state = {"slot": 0, "bank": 0}

    def stream(src, n):
        s = slots[state["slot"]]
        state["slot"] = (state["slot"] + 1) % NS
        b.dma("sp", s[:, 0:n], src, s.name)
        return s

    def bank():
        p = ps[state["bank"]]
        state["bank"] = (state["bank"] + 1) % 8
        return p

    C0 = [0]

    def W(t, lo=0, hi=0):
        return t[:, C0[0] + lo:T + hi]

    def square(hb, k):
        b.act(W(sq[k]), W(hb[k]), AF.Square)

    def rmsnorm(hb, gcol, final=False, presq=False, filler=None, dst=None, sqb=None):
        front = sqb is sq5
        Wn = (lambda t: t[:, :]) if front else W
        sqb = sq if sqb is None else sqb
        if not presq:
            for k in range(KT):
                b.act(Wn(sqb[k]), Wn(hb[k]), AF.Square)
        pb = bank()
        b.mm([(Wn(pb), ones_bf[:, :], Wn(sqb[k]), k == 0, k == KT - 1, None) for k in range(KT)])
        srt_, rstd_ = (srt5, rstd5) if front else (srt, rstd)
        b.act(Wn(srt_), Wn(pb), AF.Sqrt, bias=eps_t[:, :], scale=1.0 / D)
        if filler is not None:
            filler()
        b.recip(Wn(rstd_), Wn(srt_))
        dst = u if dst is None else dst
        for k in range(KT):
            o = Wn(hb[k]) if final else Wn(dst[k])
            b.stt(o, Wn(hb[k]), cvec[:, gcol + k:gcol + k + 1], Wn(rstd_), ALU.mult, ALU.mult)

    HSv = HS[:, :, :, :].rearrange("p c r (g j) -> p c r g j", j=4)
    P.alias["HS"] = ["HS.q0", "HS.q1", "HS.q2", "HS.q3", "HS.c0"]
    Bt4 = sb("Bt4", [128, 4, 2, 32])

    def s5_statein(qt):
        w = stream(s5scr[qt], SLOT)
        wv = w[:, :].rearrange("p (g s r m) -> p g s r m", g=2, s=L, r=2)
        pbs = [bank() for _ in range(4)]
        mms = []
        for g2 in range(2):
            gt = 2 * qt + g2
            uv = u5[gt][:, :].rearrange("p (c s) -> p c s", s=L)
            for ri in range(2):
                c0 = (ri * 2 + g2) * NCH
                for s in range(L):
                    for j in range(4):
                        mms.append((pbs[j][:, c0:c0 + NCH],
                                    wv[32 * j:32 * j + 32, g2, s, ri, :],
                                    uv[32 * j:32 * j + 32, :, s],
                                    s == 0, s == L - 1, (32 * j, 0)))
        b.mm(mms)
        for j in range(4):
            for ri in range(2):
                b.act(HSv[:, 1:NCH + 1, ri, 2 * qt:2 * qt + 2, j],
                      pbs[j][:, ri * 128:(ri + 1) * 128].rearrange("p (g c) -> p c g", g=2), AF.Copy,
                      wk=["HS.q%d" % qt])

    def s5_prefix_reduce_q(qt):
        cast_some(1)
        ps8 = slice(8 * qt, 8 * qt + 8)
        hk = ["HS.q%d" % qt]
        xv_ = lambda ri: HS[:, 1:NCH + 1, ri, ps8].rearrange("p (b k) j -> p b k j", b=4)
        pw_ = lambda ri: PW[:, :, ri, ps8].unsqueeze(1).broadcast_to([128, 4, 16, 8])
        v4 = lambda t: t[:, :].rearrange("p (b k j) -> p b k j", b=4, k=16)
        r4 = lambda t: t[:, :].rearrange("p (b k j) -> p b j k", b=4, k=16)
        t1, t2, t3, t4 = tmpf[0], tmpf[1], cbuf[0], cbuf[1]
        b.tt("dve", v4(t1), pw_(0), xv_(0), ALU.mult, rk=hk + ["PW"])
        b.tt("dve", v4(t2), pw_(1), xv_(1), ALU.mult, rk=hk + ["PW"])
        b.stt(t1[:, :], t2[:, :], -1.0, t1[:, :], ALU.mult, ALU.add)
        b.P.op("dve", lambda e, o=Bt4[:, :, 0, ps8], i=r4(t1): e.tensor_reduce(out=o, in_=i, axis=mybir.AxisListType.X, op=ALU.add),
               reads=[t1.name], writes=["Bt4.q%d" % qt])
        b.tt("pool", v4(t3), pw_(0), xv_(1), ALU.mult, rk=hk + ["PW"])
        b.tt("pool", v4(t4), pw_(1), xv_(0), ALU.mult, rk=hk + ["PW"])
        b.tt("pool", t3[:, :], t3[:, :], t4[:, :], ALU.add)
        b.P.op("dve", lambda e, o=Bt4[:, :, 1, ps8], i=r4(t3): e.tensor_reduce(out=o, in_=i, axis=mybir.AxisListType.X, op=ALU.add),
               reads=[t3.name], writes=["Bt4i.q%d" % qt])

    def s5_prefix_combine(nblk=4):
        bk = ["Bt4.q%d" % q for q in range(4)] + ["Bt4i.q%d" % q for q in range(4)]
        for blk in range(nblk):
            b.tt("dve", rt1[:, :, :], A1x[:, :, :], HS[:, 0, :, :], ALU.mult, rk=["A1x", "HS.c0"])
            b.tt("dve", rt2[:, 0, :], A2x[:, 0, :], HS[:, 0, 1, :], ALU.mult, rk=["A2x", "HS.c0"])
            b.tt("dve", rt2[:, 1, :], A2x[:, 1, :], HS[:, 0, 0, :], ALU.mult, rk=["A2x", "HS.c0"])
            b.tt("dve", rt1[:, :, :], rt1[:, :, :], rt2[:, :, :], ALU.add)
            b.tt("dve", HS[:, 0, :, :], rt1[:, :, :], Bt4[:, blk, :, :], ALU.add, rk=["rt1"] + bk, wk=["HS.c0"])

    def s5_recur(eng, c_from=1):
        for c in range(c_from, NCH + 1):
            b.tt(eng, rt1[:, :, :], A1[:, :, :], HS[:, c - 1, :, :], ALU.mult)
            b.tt(eng, rt2[:, 0, :], A2[:, 0, :], HS[:, c - 1, 1, :], ALU.mult)
            b.tt(eng, rt2[:, 1, :], A2[:, 1, :], HS[:, c - 1, 0, :], ALU.mult)
            b.tt(eng, HS[:, c, :, :], HS[:, c, :, :], rt1[:, :, :], ALU.add)
            b.tt(eng, HS[:, c, :, :], HS[:, c, :, :], rt2[:, :, :], ALU.add)

    def s5_carry(eng="pool"):
        b.copy(eng, HS[:, 0, :, :], HS[:, NCH, :, :])

    def s5_hb(eng):
        for ri in range(2):
            b.copy(eng, Hb[ri][:, :, :], HS[:, 0:NCH, ri, :].rearrange("p c j -> p j c"))
        s5_carry(eng)

    def s5_back():
        cc = C0[0] // L
        for gt in range(KT):
            w = stream(s5scr[4 + gt, :, 0:3072], 3072)
            kv = w[:, 0:1024].rearrange("p (k c) -> p k c", k=L)
            vv = w[:, 1024:3072].rearrange("p (j s r m) -> p j s r m", j=4, s=L, r=2)
            pb = bank()
            yv = pb[:, :].rearrange("p (c s) -> p c s", s=L)
            uv = u5[gt][:, :].rearrange("p (c s) -> p c s", s=L)
            mms = [(W(pb), kv[:, 0, :], W(u5[gt]), True, False, None)]
            for j in range(4):
                for s in range(L):
                    for ri in range(2):
                        mms.append((yv[32 * j:32 * j + 32, cc:NCH, s], vv[:, j, s, ri, :],
                                    Hb[ri][:, 4 * gt + j, cc:NCH], False, False, (0, 32 * j)))
            for k in range(1, L):
                if SIM_SAFE:
                    for s in range(k, L):
                        mms.append((yv[:, cc:NCH, s], kv[:, k, :], uv[:, cc:NCH, s - k], False, False, None))
                else:
                    mms.append((yv[:, cc:NCH, k:L], kv[:, k, :], uv[:, cc:NCH, 0:L - k], False, False, None))
            last = mms[-1]
            mms[-1] = (last[0], last[1], last[2], False, True, last[5])
            b.mm(mms)
            b.act(W(z[gt]), W(pb), AF.Gelu_apprx_tanh)

    def glu(hb, hook=None, add_eng="pool"):
        pend = []

        def stage_b(i, t):
            b.tt(add_eng, W(hb[i]), W(hb[i]), W(t), ALU.add)
            square(hb, i)

        for blk in range(4):
            if blk == 1 and hook is not None:
                hook()
            w = stream(wbf[B_GLU + blk], SLOT)
            wv = w[:, :].rearrange("p (i a k c) -> p i a k c", i=2, a=2, k=KT)
            for i2 in range(2):
                i = 2 * blk + i2
                pa, pg = bank(), bank()
                b.mm([(W(pa), wv[:, i2, 0, k, :], W(z[k]), k == 0, k == KT - 1, None) for k in range(KT)])
                b.mm([(W(pg), wv[:, i2, 1, k, :], W(z[k]), k == 0, k == KT - 1, None) for k in range(KT)])
                s = sbuf_[i % 2]
                b.act(W(s), W(pg), AF.Sigmoid)
                t = tmpf[i % 2]
                b.tt("dve", W(t), W(pa), W(s), ALU.mult)
                if pend:
                    stage_b(*pend.pop())
                pend.append((i, t))
        stage_b(*pend.pop())

    def ffn(hb, layer, par, filler=None, sqr=True):
        b_up = (B_UP0, B_UP1)[layer]
        b_dn = (B_DN0, B_DN1)[layer]
        halo_old, halo_new = haloF[layer][par], haloF[layer][1 - par]
        c0 = C0[0]
        pend = []

        def stage_b(f, pv):
            cb, sg = cbuf[f % 2], sbuf_[f % 2]
            bcol = C_FB + layer * FT + f
            b.act(W(sg), W(cb), AF.Silu, bias=cvec[:, bcol:bcol + 1])
            b.tt("dve", W(actb[f]), W(sg), W(pv), ALU.mult)

        for blk in range(FT // 2):
            w = stream(wbf[b_up + blk], SLOT)
            wv = w[:, :].rearrange("p (f a k c) -> p f a k c", f=2, a=2, k=KT)
            for f2 in range(2):
                f = 2 * blk + f2
                pg, pv = bank(), bank()
                b.mm([(W(pg), wv[:, f2, 0, k, :], W(u[k]), k == 0, k == KT - 1, None) for k in range(KT)])
                b.mm([(W(pv), wv[:, f2, 1, k, :], W(u[k]), k == 0, k == KT - 1, None) for k in range(KT)])
                gb, cb = gbuf[f % 2], cbuf[f % 2]
                wc = lambda kk: cvec[:, C_FW + (layer * 3 + kk) * FT + f: C_FW + (layer * 3 + kk) * FT + f + 1]
                b.act(W(gb, 2, 2), W(pg), AF.Copy)
                b.act(gb[:, c0:c0 + 2], halo_old[:, f, :], AF.Copy)
                b.act(halo_new[:, f, :], pg[:, T - 2:T], AF.Copy)
                b.act(W(cb), W(pg), AF.Copy, scale=wc(2))
                b.stt(W(cb), W(gb, 1, 1), wc(1), W(cb), ALU.mult, ALU.add)
                b.stt(W(cb), W(gb, 0, 0), wc(0), W(cb), ALU.mult, ALU.add)
                if pend:
                    stage_b(*pend.pop())
                pend.append((f, pv))
        stage_b(*pend.pop())
        if filler is not None:
            filler()
        FH = FT // 2
        for hf in range(2):
            for bi in range(4):
                w = stream(wbf[b_dn + hf * 4 + bi, :, 0:2 * FH * 128], 2 * FH * 128)
                wv = w[:, 0:2 * FH * 128].rearrange("p (m f c) -> p m f c", m=2, f=FH)
                for m2 in range(2):
                    m = 2 * bi + m2
                    pb = bank()
                    b.mm([(W(pb), wv[:, m2, f, :], W(actb[hf * FH + f]), f == 0, f == FH - 1, None)
                          for f in range(FH)])
                    b.tt("dve", W(hb[m]), W(hb[m]), W(pb), ALU.add)
                    if sqr and hf == 1:
                        square(hb, m)

    def shortconv(hb, par):
        halo_old, halo_new = haloS[par], haloS[1 - par]
        c0 = C0[0]
        for i in range(KT):
            w = stream(wbf[B_IN + i, :, 0:3072], 3072)
            wv = w[:, 0:3072].rearrange("p (a k c) -> p a k c", a=3, k=KT)
            pbg, pcg, phh = bank(), bank(), bank()
            for a, pp in ((0, pbg), (1, pcg), (2, phh)):
                b.mm([(W(pp), wv[:, a, k, :], W(u[k]), k == 0, k == KT - 1, None) for k in range(KT)])
            cg, pbf, cb = cgs[i % 2], pbuf[i % 2], cbuf[i % 2]
            b.act(W(cg), W(pcg), AF.Copy)
            b.tt("dve", W(pbf, 2, 2), W(cg), W(phh), ALU.mult)
            b.act(pbf[:, c0:c0 + 2], halo_old[:, i, :], AF.Copy)
            b.act(halo_new[:, i, :], pbf[:, T:T + 2], AF.Copy)
            wc = lambda kk: cvec[:, C_SCW + kk * KT + i: C_SCW + kk * KT + i + 1]
            b.ts("dve", W(cb), W(pbf, 2, 2), wc(2), ALU.mult)
            b.stt(W(cb), W(pbf, 1, 1), wc(1), W(cb), ALU.mult, ALU.add)
            b.stt(W(cb), W(pbf, 0, 0), wc(0), W(cb), ALU.mult, ALU.add)
            b.tt("dve", W(q[i]), W(cb), W(pbg), ALU.mult)
        for blk in range(2):
            w = stream(wbf[B_OUT + blk], SLOT)
            wv = w[:, :].rearrange("p (m k c) -> p m k c", m=4, k=KT)
            for m4 in range(4):
                m = 4 * blk + m4
                pb = bank()
                b.mm([(W(pb), wv[:, m4, k, :], W(q[k]), k == 0, k == KT - 1, None) for k in range(KT)])
                b.tt("dve", W(hb[m]), W(hb[m]), W(pb), ALU.add)
                square(hb, m)

    xv = xT.rearrange("(k p) t -> k p t", p=128)
    ov = outT.rearrange("(k p) t -> k p t", p=128)
    def load_x(ti, eng="sp"):
        hb = h[ti % 2]
        return [b.dma(eng, hb[k][:, :], xv[k, :, ti * T:(ti + 1) * T], hb[k].name) for k in range(KT)]

    def front_a(ti):
        rmsnorm(h[ti % 2], C_NM0, dst=u5, sqb=sq5)

    def front_sq(ti):
        for k in range(KT):
            b.act(sq5[k][:, :], h[ti % 2][k][:, :], AF.Square)

    def front_rest(ti):
        rmsnorm(h[ti % 2], C_NM0, dst=u5, sqb=sq5, presq=True)

    for ti in range(npre + 1):
        evs = load_x(ti)
        if ti == npre:
            cast_some(NBLK)
            for g in range(7):
                P.seal("cast%d" % g)
        front_a(ti)
        for qt in range(4):
            s5_statein(qt)
            s5_prefix_reduce_q(qt)
        if ti < npre:
            s5_prefix_combine()
        else:
            s5_prefix_combine(3)
            b.copy("dve", HS[:, 48, :, :], HS[:, 0, :, :])
            s5_recur("dve", c_from=49)
            s5_hb("dve")
    for ti in range(npre, ntile):
        hb = h[ti % 2]
        par = (ti - npre) % 2
        nxt = ti + 1 < ntile
        C0[0] = (T - WARM_COLS) if ti == npre else 0
        if ti == npre and nxt:
            s5_back()
            load_x(ti + 1)
            front_a(ti + 1)
            for qt in range(4):
                s5_statein(qt)
            s5_recur("pool")
            glu(hb, add_eng="dve")
            rmsnorm(hb, C_NF0, presq=True)
            ffn(hb, 0, par)
            rmsnorm(hb, C_NM1, presq=True)
        else:
            if nxt:
                load_x(ti + 1, "pool")
            s5_back()
            if nxt:
                front_sq(ti + 1)
            glu(hb, hook=(lambda: front_rest(ti + 1)) if nxt else None)
            rmsnorm(hb, C_NF0, presq=True, filler=(lambda: (s5_statein(0), s5_statein(1))) if nxt else None)
            ffn(hb, 0, par)
            rmsnorm(hb, C_NM1, presq=True, filler=(lambda: (s5_statein(2), s5_statein(3))) if nxt else None)
            if nxt:
                s5_recur("pool")
        shortconv(hb, par)
        rmsnorm(hb, C_NF1, presq=True)
        ffn(hb, 1, par, filler=(lambda: s5_hb("act")) if nxt else None)
        if ti > npre:
            mi = ti - npre - 1
            rmsnorm(hb, C_NFIN, final=True, presq=True)
            for k in range(KT):
                b.dma("pool", ov[k, :, mi * T:(mi + 1) * T], hb[k][:, :], "o" + hb[k].name)
    P.emit()
    return nc


def _cols(v):
    return np.ascontiguousarray(v.reshape(-1, 128).T)


def prepare_shared(inp):
    f32 = np.float32
    wst = np.zeros((NBLK, 128, SLOT), f32)
    wg = inp["s5_w_glu"][0]
    for blk in range(4):
        v = wst[B_GLU + blk].reshape(128, 2, 2, KT, 128)
        for i2 in range(2):
            i = 2 * blk + i2
            for a in range(2):
                c0 = a * D + i * 128
                v[:, i2, a] = wg[:, c0:c0 + 128].reshape(KT, 128, 128).transpose(1, 0, 2)
    for layer in range(2):
        wu = inp["ffn_w_up"][layer]
        wd = inp["ffn_w_down"][layer]
        b_up = (B_UP0, B_UP1)[layer]
        b_dn = (B_DN0, B_DN1)[layer]
        for blk in range(FT // 2):
            v = wst[b_up + blk].reshape(128, 2, 2, KT, 128)
            for f2 in range(2):
                f = 2 * blk + f2
                for a in range(2):
                    c0 = a * FF + f * 128
                    v[:, f2, a] = wu[:, c0:c0 + 128].reshape(KT, 128, 128).transpose(1, 0, 2)
        FH = FT // 2
        for hf in range(2):
            for bi in range(4):
                v = wst[b_dn + hf * 4 + bi][:, 0:2 * FH * 128].reshape(128, 2, FH, 128)
                for m2 in range(2):
                    m = 2 * bi + m2
                    v[:, m2] = wd[hf * FH * 128:(hf + 1) * FH * 128, m * 128:(m + 1) * 128].reshape(FH, 128, 128).transpose(1, 0, 2)
    wi = inp["sc_w_in"][0]
    wo = inp["sc_w_out"][0]
    for i in range(KT):
        v = wst[B_IN + i][:, 0:3072].reshape(128, 3, KT, 128)
        for a in range(3):
            c0 = a * D + i * 128
            v[:, a] = wi[:, c0:c0 + 128].reshape(KT, 128, 128).transpose(1, 0, 2)
    for blk in range(2):
        v = wst[B_OUT + blk].reshape(128, 4, KT, 128)
        for m4 in range(4):
            m = 4 * blk + m4
            v[:, m4] = wo[:, m * 128:(m + 1) * 128].reshape(KT, 128, 128).transpose(1, 0, 2)

    cvec = np.zeros((128, NCV), f32)
    cvec[:, C_NM0:C_NM0 + 8] = _cols(inp["norm_mix"][0])
    cvec[:, C_NM1:C_NM1 + 8] = _cols(inp["norm_mix"][1])
    cvec[:, C_NF0:C_NF0 + 8] = _cols(inp["norm_ffn"][0])
    cvec[:, C_NF1:C_NF1 + 8] = _cols(inp["norm_ffn"][1])
    cvec[:, C_NFIN:C_NFIN + 8] = _cols(inp["norm_final"])
    cvec[:, C_D:C_D + 8] = _cols(inp["s5_d"][0])
    for kk in range(3):
        cvec[:, C_SCW + kk * KT:C_SCW + (kk + 1) * KT] = _cols(inp["sc_conv_w"][0, kk])
    for layer in range(2):
        for kk in range(3):
            c0 = C_FW + (layer * 3 + kk) * FT
            cvec[:, c0:c0 + FT] = _cols(inp["ffn_conv_w"][layer, kk])
        c0 = C_FB + layer * FT
        cvec[:, c0:c0 + FT] = _cols(inp["ffn_conv_b"][layer])

    cst = np.zeros((128, 264), f32)
    pidx = np.arange(128)
    cst[:, 0:128] = (pidx[:, None] // 16 == pidx[None, :] // 16)
    cst[:, 128:256] = np.eye(128)
    cst[:, 256] = ((pidx // 16) % 2 == 0)
    cst[:, 257] = ((pidx // 16) % 2 == 1)
    cst[:, 258] = (pidx < 64)
    cst[:, 259] = (pidx >= 64)
    cst[:, 260] = -1.0 * (pidx < 64)
    cst[:, 261] = -1.0 * (pidx >= 64)

    a_re, a_im, ldt = inp["s5_a_re"][0], inp["s5_a_im"][0], inp["s5_log_dt"][0]
    b_re, b_im = inp["s5_b_re"][0], inp["s5_b_im"][0]
    c_re, c_im = inp["s5_c_re"][0], inp["s5_c_im"][0]

    def layA_b(x):
        return np.ascontiguousarray(x.reshape(8, 8, 64, 16).transpose(1, 3, 0, 2).reshape(128, 512))

    def layA_a(x):
        y = x.reshape(8, 8, 64).transpose(1, 0, 2)
        return np.ascontiguousarray(np.broadcast_to(y[:, None], (8, 16, 8, 64)).reshape(128, 512))

    ldt_gp = np.broadcast_to(ldt[:, None], (64, 64))
    s5A = np.stack([layA_b(b_re), layA_b(b_im), layA_a(a_re), layA_a(a_im), layA_a(ldt_gp)]).astype(f32)

    def layP_a(x):
        return np.ascontiguousarray(x.reshape(32, 2, 64).transpose(1, 2, 0).reshape(128, 32))

    def layP_c(x):
        return np.ascontiguousarray(x.reshape(32, 2, 16, 64).transpose(1, 3, 0, 2).reshape(128, 512))

    s5Ps = np.stack([layP_a(a_re), layP_a(a_im), layP_a(ldt_gp)]).astype(f32)
    s5Pc = np.stack([layP_c(c_re), layP_c(c_im)]).astype(f32)
    s5Qs = np.stack([a_re.T, a_im.T, ldt_gp.T]).astype(f32)
    s5Qb = np.stack([b_re.transpose(1, 0, 2).reshape(64, 1024), b_im.transpose(1, 0, 2).reshape(64, 1024),
                     c_re.transpose(2, 0, 1).reshape(64, 1024), c_im.transpose(2, 0, 1).reshape(64, 1024)]).astype(f32)
    return {"wst": wst, "cvec": cvec, "cst": cst, "s5A": np.ascontiguousarray(s5A), "s5Ps": np.ascontiguousarray(s5Ps),
            "s5Pc": np.ascontiguousarray(s5Pc), "s5Qs": np.ascontiguousarray(s5Qs), "s5Qb": np.ascontiguousarray(s5Qb)}


NPRE = 7
NMAIN = 8
SIM_SAFE = False
WARM_COLS = 16
STOP = None


def kernel(**inputs):
    inp = {k: np.asarray(v) for k, v in inputs.items()}
    x = inp["x"].astype(np.float32, copy=False)
    bsz, seq, d = x.shape
    half = seq // 2
    shared = prepare_shared(inp)
    in_maps = []
    for c in range(8):
        bi, hf = c // 2, c % 2
        xt = np.zeros((D, 2 * half), np.float32)
        if hf == 1:
            xt[:, 0:half] = x[bi, 0:half].T
        xt[:, half:] = x[bi, hf * half:(hf + 1) * half].T
        m = dict(shared)
        m["xT"] = xt
        in_maps.append(m)
    nc = build_program(NPRE, NMAIN)
    res = run_bass_kernel_spmd(nc, in_maps, core_ids=list(range(8)))
    out = np.empty((bsz, seq, d), np.float32)
    for c in range(8):
        bi, hf = c // 2, c % 2
        out[bi, hf * half:(hf + 1) * half] = res.results[c]["outT"].T
    return out
```
